# Optimizing a Trainium2 kernel written in Bass

```python
import math
import jax, jax.numpy as jnp
from jax import lax
import numpy as np

D_MODEL = 2048
BATCH = 8
SEQ = 4096
DEPTH = 4

D_MIX = D_MODEL
FOURIER_GROUPS = 4
FOURIER_GROUP_DIM = D_MODEL // 16
D_FOURIER = FOURIER_GROUPS * FOURIER_GROUP_DIM
D_HGRN = D_MIX - D_FOURIER
HG_DIM = 128
HG_HEADS = D_HGRN // HG_DIM
D_IN = D_FOURIER + 5 * D_HGRN
CHUNK = 64
D_FF = ((8 * D_MODEL // 3 + 127) // 128) * 128
ALPHA = (2 * DEPTH) ** 0.25
BETA = (8 * DEPTH) ** -0.25
N_MOD = 9
ADA_INIT = 0.2
LN_EPS = 1e-5
RMS_EPS = 1e-6

kernel_name = "fourier_hgrn2_macaron_deepnorm_adaln"


def layer_norm(x, g, b):
    xf = x.astype(jnp.float32)
    mu = jnp.mean(xf, axis=-1, keepdims=True)
    var = jnp.mean(jnp.square(xf - mu), axis=-1, keepdims=True)
    y = (xf - mu) * lax.rsqrt(var + LN_EPS) * g.astype(jnp.float32) + b.astype(jnp.float32)
    return y.astype(x.dtype)


def rms_norm_heads(y, gain, n_heads):
    B, S, W = y.shape
    yf = y.astype(jnp.float32).reshape(B, S, n_heads, W // n_heads)
    yf = yf * lax.rsqrt(jnp.mean(jnp.square(yf), axis=-1, keepdims=True) + RMS_EPS)
    return yf.reshape(B, S, W) * gain.astype(jnp.float32)


def modulate(x, shift, scale):
    return x * (1.0 + scale[:, None, :]) + shift[:, None, :]


def swiglu(h, w_in, w_out):
    a, b = jnp.split(h @ w_in, 2, axis=-1)
    return (jax.nn.silu(a) * b) @ w_out


def fourier_mix(u):
    B, S, _ = u.shape
    uf = u.astype(jnp.float32).reshape(B, S, FOURIER_GROUPS, FOURIER_GROUP_DIM)
    y = jnp.fft.fft2(uf, axes=(1, 3), norm="ortho").real
    return y.reshape(B, S, D_FOURIER)


def gla_chunked(q, k, v, log_f):
    B, S, H, K = q.shape
    V = v.shape[-1]
    N = S // CHUNK

    def blocks(t):
        return t.reshape(B, N, CHUNK, H, t.shape[-1]).transpose(1, 0, 3, 2, 4)

    q, k, v, log_f = blocks(q), blocks(k), blocks(v), blocks(log_f)
    b = jnp.cumsum(log_f, axis=3)
    b_ref = b[:, :, :, CHUNK // 2:CHUNK // 2 + 1]
    q_in = q * jnp.exp(b - b_ref)
    k_in = k * jnp.exp(b_ref - b)
    scores = jnp.einsum('nbhtk,nbhsk->nbhts', q_in, k_in)
    mask = jnp.tril(jnp.ones((CHUNK, CHUNK), dtype=bool))
    o_intra = jnp.einsum('nbhts,nbhsv->nbhtv', jnp.where(mask, scores, 0.0), v)
    b_last = b[:, :, :, -1:]
    q_st = q * jnp.exp(b)
    k_st = k * jnp.exp(b_last - b)
    decay = jnp.exp(b_last[:, :, :, 0])

    def step(state, xs):
        q_c, k_c, v_c, d_c = xs
        o_c = jnp.einsum('bhtk,bhkv->bhtv', q_c, state)
        state = d_c[..., None] * state + jnp.einsum('bhsk,bhsv->bhkv', k_c, v_c)
        return state, o_c

    state0 = jnp.zeros((B, H, K, V), jnp.float32)
    _, o_inter = lax.scan(step, state0, (q_st, k_st, v, decay))
    o = o_intra + o_inter
    return o.transpose(1, 0, 3, 2, 4).reshape(B, S, H, V)


def hgrn2_direction(q, v, z_f, lb):
    f = lb + (1.0 - lb) * jax.nn.sigmoid(z_f)
    return gla_chunked(q, 1.0 - f, v, jnp.log(f))


def bidirectional_hgrn2(q_raw, v_raw, zf_fwd, zf_bwd, lb_fwd, lb_bwd):
    B, S, _ = q_raw.shape

    def heads(t):
        return t.astype(jnp.float32).reshape(B, S, HG_HEADS, HG_DIM)

    def flip(t):
        return jnp.flip(t, axis=1)

    q = jax.nn.silu(heads(q_raw))
    v = heads(v_raw)
    lb_f = lb_fwd.reshape(HG_HEADS, HG_DIM)
    lb_b = lb_bwd.reshape(HG_HEADS, HG_DIM)
    o_fwd = hgrn2_direction(q, v, heads(zf_fwd), lb_f)
    o_bwd = flip(hgrn2_direction(flip(q), flip(v), flip(heads(zf_bwd)), lb_b))
    return (o_fwd + o_bwd).reshape(B, S, D_HGRN)


def token_mixer(h, w_in, lb_fwd, lb_bwd, fourier_g, hgrn_g, w_out):
    proj = h @ w_in
    cuts = [D_FOURIER + j * D_HGRN for j in range(5)]
    u_f, q_raw, v_raw, zf_fwd, zf_bwd, z_gate = jnp.split(proj, cuts, axis=-1)
    y_f = rms_norm_heads(fourier_mix(u_f), fourier_g, FOURIER_GROUPS)
    o_h = bidirectional_hgrn2(q_raw, v_raw, zf_fwd, zf_bwd, lb_fwd, lb_bwd)
    y_h = rms_norm_heads(o_h, hgrn_g, HG_HEADS) * jax.nn.silu(z_gate.astype(jnp.float32))
    y = jnp.concatenate([y_f, y_h], axis=-1).astype(h.dtype)
    return y @ w_out


def setup_inputs(seed: int = 0) -> dict:
    key = jax.random.key(seed)
    ks = jax.random.split(key, 24)

    def nrm(k, shape, scale):
        return jax.random.normal(k, shape, jnp.float32) * scale

    return {
        "x": nrm(ks[0], (BATCH, SEQ, D_MODEL), 1.0),
        "c": nrm(ks[1], (BATCH, D_MODEL), 1.0),
        "w_ada": nrm(ks[2], (DEPTH, D_MODEL, N_MOD * D_MODEL), ADA_INIT * D_MODEL ** -0.5),
        "b_ada": nrm(ks[3], (DEPTH, N_MOD * D_MODEL), 0.01),
        "w_ffn1_in": nrm(ks[4], (DEPTH, D_MODEL, 2 * D_FF), D_MODEL ** -0.5),
        "w_ffn1_out": nrm(ks[5], (DEPTH, D_FF, D_MODEL), BETA * D_FF ** -0.5),
        "ln1_g": 1.0 + nrm(ks[6], (DEPTH, D_MODEL), 0.02),
        "ln1_b": nrm(ks[7], (DEPTH, D_MODEL), 0.02),
        "w_in": nrm(ks[8], (DEPTH, D_MODEL, D_IN), D_MODEL ** -0.5),
        "lower_bounds": nrm(ks[9], (DEPTH, 2, D_HGRN), 0.1),
        "fourier_g": 1.0 + nrm(ks[10], (DEPTH, D_FOURIER), 0.02),
        "hgrn_g": 1.0 + nrm(ks[11], (DEPTH, D_HGRN), 0.02),
        "w_out": nrm(ks[12], (DEPTH, D_MIX, D_MODEL), BETA * D_MIX ** -0.5),
        "ln2_g": 1.0 + nrm(ks[13], (DEPTH, D_MODEL), 0.02),
        "ln2_b": nrm(ks[14], (DEPTH, D_MODEL), 0.02),
        "w_ffn2_in": nrm(ks[15], (DEPTH, D_MODEL, 2 * D_FF), D_MODEL ** -0.5),
        "w_ffn2_out": nrm(ks[16], (DEPTH, D_FF, D_MODEL), BETA * D_FF ** -0.5),
        "ln3_g": 1.0 + nrm(ks[17], (DEPTH, D_MODEL), 0.02),
        "ln3_b": nrm(ks[18], (DEPTH, D_MODEL), 0.02),
    }


def reference(x, c, w_ada, b_ada, w_ffn1_in, w_ffn1_out, ln1_g, ln1_b, w_in, lower_bounds,
              fourier_g, hgrn_g, w_out, ln2_g, ln2_b, w_ffn2_in, w_ffn2_out, ln3_g, ln3_b):
    lb_soft = jax.nn.softmax(lower_bounds.astype(jnp.float32), axis=0)
    lbs = jnp.cumsum(lb_soft, axis=0) - lb_soft[0:1]
    c_act = jax.nn.silu(c)
    for l in range(DEPTH):
        ada = (c_act @ w_ada[l] + b_ada[l]).astype(x.dtype)
        sh1, sc1, g1, sh2, sc2, g2, sh3, sc3, g3 = jnp.split(ada, N_MOD, axis=-1)
        y = swiglu(modulate(x, sh1, sc1), w_ffn1_in[l], w_ffn1_out[l])
        x = layer_norm(ALPHA * x + 0.5 * (1.0 + g1[:, None, :]) * y, ln1_g[l], ln1_b[l])
        y = token_mixer(modulate(x, sh2, sc2), w_in[l], lbs[l, 0], lbs[l, 1],
                        fourier_g[l], hgrn_g[l], w_out[l])
        x = layer_norm(ALPHA * x + (1.0 + g2[:, None, :]) * y, ln2_g[l], ln2_b[l])
        y = swiglu(modulate(x, sh3, sc3), w_ffn2_in[l], w_ffn2_out[l])
        x = layer_norm(ALPHA * x + 0.5 * (1.0 + g3[:, None, :]) * y, ln3_g[l], ln3_b[l])
    return x
```

```python
import numpy as np
import ml_dtypes
from contextlib import ExitStack
import concourse.bass as bass
import concourse.mybir as mybir
from concourse.bass_utils import run_bass_kernel_spmd

F32 = mybir.dt.float32
BF16 = mybir.dt.bfloat16
ALU = mybir.AluOpType
AF = mybir.ActivationFunctionType

D = 2048
NKC = 16
DFF = 5504
NM = 43
DIN = 8192
DEPTH = 4
NH = 12
NU = 16
CH = 64
TS = 512
ALPHA = (2 * DEPTH) ** 0.25
LN_EPS = 1e-5
RMS_EPS = 1e-6


class CX:
    def __init__(self, nc):
        self.nc = nc
        self.eng = {"pe": nc.tensor, "act": nc.scalar, "dve": nc.vector, "pool": nc.gpsimd, "sp": nc.sync}
        self.sem = {}
        self.cnt = {}
        for e in self.eng:
            self.sem[e] = nc.semaphore("sem_" + e).__enter__()
            self.cnt[e] = 0
        self.KQ = 12
        self.dq = {q: [nc.semaphore("dq_%s_%d" % (q, i)).__enter__() for i in range(self.KQ)] for q in ("sp", "pool")}
        self.dqi = {"sp": 0, "pool": 0}
        self.waited = {e: {} for e in self.eng}
        self.lastw = {}
        self.readers = {}
        self.allsems = {}

    def _wait(self, e, tok):
        key, sem, val, src = tok
        if self.waited[e].get(key, 0) >= val:
            return
        self.eng[e].wait_ge(sem, val)
        self.waited[e][key] = val

    def _deps(self, e, reads, writes):
        toks = []
        for k in list(reads) + list(writes):
            t = self.lastw.get(k)
            if t is not None:
                toks.append(t)
        for k in writes:
            r = self.readers.get(k)
            if r:
                toks.extend(r.values())
        for t in toks:
            if t[3] == "pe" and e == "pe":
                continue
            self._wait(e, t)

    def _record(self, tok, reads, writes):
        for k in writes:
            self.lastw[k] = tok
            self.readers[k] = {}
        for k in reads:
            d = self.readers.setdefault(k, {})
            d[tok[0]] = tok
        self.allsems[tok[0]] = tok

    def op(self, e, fn, reads=(), writes=()):
        self._deps(e, reads, writes)
        ins = fn(self.eng[e])
        self.cnt[e] += 1
        ins.then_inc(self.sem[e], 1)
        tok = ("E" + e, self.sem[e], self.cnt[e], e)
        self._record(tok, reads, writes)
        return tok

    def dma(self, q, out, in_, reads=(), writes=()):
        i = self.dqi[q]
        self.dqi[q] += 1
        K = self.KQ
        sem = self.dq[q][i % K]
        key = "Q%s%d" % (q, i % K)
        if i >= K:
            self._wait(q, (key, sem, 16 * (i // K), "dma"))
        self._deps(q, reads, writes)
        ins = self.eng[q].dma_start(out=out, in_=in_)
        ins.then_inc(sem, 16)
        tok = (key, sem, 16 * (i // K + 1), "dma")
        self._record(tok, reads, writes)
        return tok

    def barrier(self):
        toks = list(self.allsems.values())
        for e in self.eng:
            for t in toks:
                self._wait(e, t)
        self.lastw = {}
        self.readers = {}

    def finish(self):
        for t in list(self.allsems.values()):
            self._wait("sp", t)


def _consts(S):
    cf = np.zeros((128, 128 + 512 + 128 + 8), np.float32)
    cf[:, 0:128] = np.eye(128, dtype=np.float32)
    rm = np.ones((512,), np.float32)
    rm[::CH] = 0.0
    cf[:, 128:640] = rm[None, :]
    cf[:, 640:768] = 1.0 / 128.0
    cf[:, 768] = 1.0
    cf[:, 769] = LN_EPS
    cf[:, 770] = RMS_EPS
    s = np.arange(128)[:, None]
    t = np.arange(128)[None, :]
    same = (s // CH) == (t // CH)
    mF = (same & (s <= t)).astype(np.float32)
    mB = (same & (s >= t)).astype(np.float32)
    cb = np.zeros((128, 512 + 512 + 256 + 128), np.float32)
    cb[:, 0:512] = np.tile(mF, (1, 4))
    cb[:, 512:1024] = np.tile(mB, (1, 4))
    dd = np.arange(128)
    ang = 2.0 * np.pi * ((dd[:, None] * dd[None, :]) % 128) / 128.0
    cb[:, 1024:1152] = np.cos(ang)
    cb[:, 1152:1280] = -np.sin(ang)
    cb[:, 1280:1408] = np.eye(128)
    tt = np.arange(S, dtype=np.int64)
    angs = 2.0 * np.pi * ((tt[:, None] * tt[None, :]) % S) / S
    dc = np.cos(angs).astype(ml_dtypes.bfloat16)
    ds = np.sin(angs).astype(ml_dtypes.bfloat16)
    return cf, cb.astype(ml_dtypes.bfloat16), dc, ds


def build(S, plan, layers_ada):
    NT = S // TS
    nc = bass.Bass("TRN2", target_bir_lowering=False)
    cx = CX(nc)

    def din(name, shape, dt=F32):
        return nc.dram_tensor(name, shape, dt, kind="ExternalInput").ap()

    x_in = din("x", [S, D])
    c_in = din("c", [1, D])
    w_ada = din("w_ada", [DEPTH, D, 9 * D])
    b_ada = din("b_ada", [DEPTH, 9 * D])
    w_f1i = din("w_ffn1_in", [DEPTH, D, 2 * DFF])
    w_f1o = din("w_ffn1_out", [DEPTH, DFF, D])
    ln_g = [din("ln1_g", [DEPTH, D]), din("ln2_g", [DEPTH, D]), din("ln3_g", [DEPTH, D])]
    ln_b = [din("ln1_b", [DEPTH, D]), din("ln2_b", [DEPTH, D]), din("ln3_b", [DEPTH, D])]
    w_in = din("w_in", [DEPTH, D, DIN])
    lowb = din("lower_bounds", [DEPTH, 2, NH * 128])
    fou_g = din("fourier_g", [DEPTH, 512])
    hg_g = din("hgrn_g", [DEPTH, NH * 128])
    w_out = din("w_out", [DEPTH, D, D])
    w_f2i = din("w_ffn2_in", [DEPTH, D, 2 * DFF])
    w_f2o = din("w_ffn2_out", [DEPTH, DFF, D])
    cf_in = din("cst_f", [128, 776])
    cb_in = din("cst_b", [128, 1408], BF16)
    dftc = din("dft_c", [S, S], BF16)
    dfts = din("dft_s", [S, S], BF16)
    y_out = nc.dram_tensor("y", [S, D], F32, kind="ExternalOutput").ap()

    def dscr(name, shape, dt=BF16):
        return nc.dram_tensor(name, shape, dt, kind="Internal").ap()

    xs = dscr("xs", [S, D], F32)
    hsc = dscr("hsc", [128, NKC, S])
    ysc = dscr("ysc", [128, NU, S])
    used_layers = sorted(set(p[1] for p in plan))
    wi_s = {}
    wo_s = {}
    for l in used_layers:
        for w in (1, 3):
            if ("ffn", l, w) in plan:
                wi_s[(l, w)] = dscr("wi_s%d_%d" % (l, w), [NM, 128, NKC, 256])
                wo_s[(l, w)] = dscr("wo_s%d_%d" % (l, w), [NKC, 128, NM, 128])
    win_s = {l: dscr("win_s%d" % l, [64, 128, NKC, 128]) for l in used_layers if ("mix", l) in plan}
    wout_s = {l: dscr("wout_s%d" % l, [128, NKC, D]) for l in used_layers if ("mix", l) in plan}

    def cast_list(kind, l, w=None):
        lst = []
        if kind == "ffn":
            wi = (w_f1i if w == 1 else w_f2i)[l].rearrange("(k p) n -> p k n", p=128)
            wo = (w_f1o if w == 1 else w_f2o)[l].rearrange("(k p) n -> p k n", p=128)
            for m in range(NM):
                def f(m=m):
                    cx.dma("pool", wi_s[(l, w)][m][:, :, 0:128], wi[:, :, m * 128:(m + 1) * 128], writes=[("wi", l, w, m)])
                    cx.dma("pool", wi_s[(l, w)][m][:, :, 128:256], wi[:, :, DFF + m * 128:DFF + (m + 1) * 128], writes=[("wi", l, w, m)])
                lst.append(f)
            for mo in range(NKC):
                def f(mo=mo):
                    cx.dma("pool", wo_s[(l, w)][mo], wo[:, :, mo * 128:(mo + 1) * 128], writes=[("wo", l, w, mo)])
                lst.append(f)
        else:
            wi = w_in[l].rearrange("(k p) n -> p k n", p=128)
            for cc in range(64):
                def f(cc=cc):
                    cx.dma("pool", win_s[l][cc], wi[:, :, cc * 128:(cc + 1) * 128], writes=[("win", l, cc)])
                lst.append(f)
            wo = w_out[l].rearrange("(k p) n -> p k n", p=128)
            for k4 in range(4):
                def f(k4=k4):
                    for c4 in range(4):
                        cx.dma("pool", wout_s[l][:, k4 * 4:(k4 + 1) * 4, c4 * 512:(c4 + 1) * 512], wo[:, k4 * 4:(k4 + 1) * 4, c4 * 512:(c4 + 1) * 512], writes=[("wout", l, k4)])
                lst.append(f)
        return lst

    pending_casts = []

    def tick(n=1):
        for _ in range(n):
            if pending_casts:
                pending_casts.pop(0)[1]()

    uid = [0]

    def psum_tensor(name, shape, dt):
        uid[0] += 1
        return nc.psum_tensor("%s_%d" % (name, uid[0]), shape, dt)

    with ExitStack() as gs:
        def sb(name, shape, dt=F32, stack=gs):
            uid[0] += 1
            return stack.enter_context(nc.sbuf_tensor("%s_%d" % (name, uid[0]), shape, dt))

        cf = sb("cf", [128, 776])
        cb = sb("cb", [128, 1408], BF16)
        ident = cf[:, 0:128]
        rmask = cf[:, 128:640]
        onesm = cf[:, 640:768]
        one11 = cf[0:1, 768:769]
        lneps = cf[:, 769:770]
        rmseps = cf[:, 770:771]
        maskF = cb[:, 0:512]
        maskB = cb[:, 512:1024]
        csg = cb[:, 1024:1280]
        identb = cb[:, 1280:1408]
        mod = sb("mod", [128, DEPTH, 144])
        lbc = sb("lbc", [128, DEPTH, 24])
        omlc = sb("omlc", [128, DEPTH, 24])
        gcol = sb("gcol", [128, 64])
        cx.dma("sp", cf[:], cf_in, writes=["cf"])
        cx.dma("sp", cb[:], cb_in, writes=["cb"])

        with ExitStack() as ps:
            cact = sb("cact", [128, NKC], F32, ps)
            crow = sb("crow", [16, 128], F32, ps)
            lbr = sb("lbr", [96, 128], F32, ps)
            gr = sb("gr", [64, 128], F32, ps)
            ex = sb("lbex", [128, 96], F32, ps)
            tot = sb("lbtot", [128, 24], F32, ps)
            wst = [sb("wst%d" % i, [128, NKC, 512], F32, ps) for i in range(2)]
            bst = [sb("bst%d" % i, [1, 512], F32, ps) for i in range(2)]
            pp = ps.enter_context(psum_tensor("pp", [128, 512], F32))
            pa = ps.enter_context(psum_tensor("pa", [128, 512], F32))
            cx.dma("sp", crow[:], c_in[0].rearrange("(j p) -> j p", p=128), writes=["crow"])
            cx.dma("sp", lbr[:], lowb.rearrange("l d (h k) -> (l d h) k", k=128), writes=["lbr"])
            cx.dma("sp", gr[0:16, :], fou_g.rearrange("l (g k) -> (l g) k", k=128), writes=["gr"])
            cx.dma("sp", gr[16:64, :], hg_g.rearrange("l (h k) -> (l h) k", k=128), writes=["gr"])
            cx.op("pe", lambda e: e.transpose(out=pp[:, 0:16], in_=crow[:], identity=ident[0:16, 0:16]), reads=["crow", "cf"], writes=["pp"])
            cx.op("act", lambda e: e.activation(out=cact[:], in_=pp[:, 0:16], func=AF.Silu), reads=["pp"], writes=["cact"])
            cx.op("pe", lambda e: e.transpose(out=pp[:, 0:96], in_=lbr[:], identity=ident[0:96, 0:96]), reads=["lbr", "cf"], writes=["pp"])
            cx.op("act", lambda e: e.activation(out=ex[:], in_=pp[:, 0:96], func=AF.Exp), reads=["pp"], writes=["ex"])
            cx.op("pe", lambda e: e.transpose(out=pp[:, 0:64], in_=gr[:], identity=ident[0:64, 0:64]), reads=["gr", "cf"], writes=["pp"])
            cx.op("dve", lambda e: e.tensor_copy(out=gcol[:], in_=pp[:, 0:64]), reads=["pp"], writes=["gcol"])
            cx.op("dve", lambda e: e.tensor_tensor(out=tot[:], in0=ex[:, 0:24], in1=ex[:, 24:48], op=ALU.add), reads=["ex"], writes=["tot"])
            cx.op("dve", lambda e: e.tensor_tensor(out=tot[:], in0=tot[:], in1=ex[:, 48:72], op=ALU.add), reads=["ex", "tot"], writes=["tot"])
            cx.op("dve", lambda e: e.tensor_tensor(out=tot[:], in0=tot[:], in1=ex[:, 72:96], op=ALU.add), reads=["ex", "tot"], writes=["tot"])
            cx.op("dve", lambda e: e.reciprocal(out=tot[:], in_=tot[:]), reads=["tot"], writes=["tot"])
            for l in range(DEPTH):
                cx.op("dve", lambda e, l=l: e.tensor_tensor(out=ex[:, l * 24:(l + 1) * 24], in0=ex[:, l * 24:(l + 1) * 24], in1=tot[:], op=ALU.mult), reads=["ex", "tot"], writes=["ex"])
            cx.op("dve", lambda e: e.memset(lbc[:, 0, :], 0.0), writes=["lbc"])
            for l in range(1, DEPTH):
                cx.op("dve", lambda e, l=l: e.tensor_tensor(out=lbc[:, l, :], in0=lbc[:, l - 1, :], in1=ex[:, l * 24:(l + 1) * 24], op=ALU.add), reads=["ex", "lbc"], writes=["lbc"])
            cx.op("dve", lambda e: e.tensor_scalar(out=omlc[:], in0=lbc[:], scalar1=-1.0, scalar2=1.0, op0=ALU.mult, op1=ALU.add), reads=["lbc"], writes=["omlc"])
            adarow = [sb("adarow%d" % i, [1, 512], F32, ps) for i in range(2)]
            prow = [ps.enter_context(psum_tensor("prow%d" % i, [128, 512], F32)) for i in range(2)]
            for l in layers_ada:
                wv = w_ada[l].rearrange("(k p) n -> p k n", p=128)
                for sl in range(36):
                    slot = sl % 2
                    cx.dma("sp", wst[slot][:], wv[:, :, sl * 512:(sl + 1) * 512], writes=[("wst", slot)])
                    cx.dma("sp", bst[slot][:], b_ada[l:l + 1, sl * 512:(sl + 1) * 512], writes=[("bst", slot)])
                    pr = prow[slot]
                    for k in range(NKC):
                        cx.op("pe", lambda e, k=k: e.matmul(pr[0:1, :], lhsT=cact[:, k:k + 1], rhs=wst[slot][:, k, :], start=(k == 0), stop=False),
                              reads=[("wst", slot), "cact"], writes=[("prow", slot)])
                    cx.op("pe", lambda e: e.matmul(pr[0:1, :], lhsT=one11, rhs=bst[slot][0:1, :], start=False, stop=True),
                          reads=[("bst", slot), "cf"], writes=[("prow", slot)])
                    cx.op("act", lambda e: e.activation(out=adarow[slot][:], in_=pr[0:1, :], func=AF.Copy), reads=[("prow", slot)], writes=[("adarow", slot)])
                    for cc in range(4):
                        col = sl * 4 + cc
                        cx.op("pe", lambda e, cc=cc, col=col: e.matmul(pa[:, col:col + 1], lhsT=adarow[slot][0:1, cc * 128:(cc + 1) * 128], rhs=one11, start=True, stop=True),
                              reads=[("adarow", slot), "cf"], writes=["pa"])
                cx.op("dve", lambda e, l=l: e.tensor_copy(out=mod[:, l, :], in_=pa[:, 0:144]), reads=["pa"], writes=["mod"])
                for j, (a, b) in {1: (1.0, 1.0), 4: (1.0, 1.0), 7: (1.0, 1.0), 2: (0.5, 0.5), 8: (0.5, 0.5), 5: (1.0, 1.0)}.items():
                    cx.op("dve", lambda e, l=l, j=j, a=a, b=b: e.tensor_scalar(out=mod[:, l, j * 16:(j + 1) * 16], in0=mod[:, l, j * 16:(j + 1) * 16], scalar1=a, scalar2=b, op0=ALU.mult, op1=ALU.add),
                          reads=["mod"], writes=["mod"])
            cx.barrier()

        def resid_ln_tail(l, lni, mo, psY, gcols, xt_slot, xt, yg, psT, lng, lnb, st, mv, rs, dst, tt):
            ygs = yg[mo % 2]
            cx.op("act", lambda e: e.activation(out=ygs[:], in_=psY[:], func=AF.Identity, scale=gcols[:, mo:mo + 1]), reads=[("psY", id(psY))], writes=[("yg", mo % 2)])
            pt = psT[mo % 2]
            for j in range(4):
                cx.op("pe", lambda e, j=j: e.transpose(out=pt[:, j * 128:(j + 1) * 128], in_=ygs[:, j * 128:(j + 1) * 128], identity=ident), reads=[("yg", mo % 2), "cf"], writes=[("psT", mo % 2)])
            xv = xt[xt_slot][:, :, mo * 128:(mo + 1) * 128]
            cx.op("dve", lambda e: e.scalar_tensor_tensor(out=xv, in0=xv, scalar=float(ALPHA), in1=pt[:].rearrange("p (j c) -> p j c", c=128), op0=ALU.mult, op1=ALU.add),
                  reads=[("psT", mo % 2), ("xt", xt_slot)], writes=[("xt", xt_slot)])
            if mo < NKC - 1:
                return None
            return ln_gen(xt_slot, xt, lng, lnb, st, mv, rs, dst, tt)

        def ln_gen(xt_slot, xt, lng, lnb, st, mv, rs, dst, tt):
            for j in range(4):
                xr = xt[xt_slot][:, j, :]
                for i in range(4):
                    cx.op("dve", lambda e, i=i: e.bn_stats(out=st[:, i * 6:(i + 1) * 6], in_=xr[:, i * 512:(i + 1) * 512]), reads=[("xt", xt_slot)], writes=["st"])
                    yield
                cx.op("dve", lambda e: e.bn_aggr(out=mv[:], in_=st[:]), reads=["st"], writes=["mv"])
                yield
                cx.op("act", lambda e: e.activation(out=rs[:], in_=mv[:, 1:2], func=AF.Sqrt, bias=lneps, scale=1.0), reads=["mv", "cf"], writes=["rs"])
                yield
                cx.op("dve", lambda e: e.reciprocal(out=rs[:], in_=rs[:]), reads=["rs"], writes=["rs"])
                yield
                cx.op("dve", lambda e: e.tensor_scalar(out=xr, in0=xr, scalar1=mv[:, 0:1], scalar2=rs[:, 0:1], op0=ALU.subtract, op1=ALU.mult), reads=["mv", "rs", ("xt", xt_slot)], writes=[("xt", xt_slot)])
                yield
                cx.op("dve", lambda e: e.tensor_tensor(out=xr, in0=xr, in1=lng[:], op=ALU.mult), reads=["lng", ("xt", xt_slot)], writes=[("xt", xt_slot)])
                yield
                cx.op("dve", lambda e: e.tensor_tensor(out=xr, in0=xr, in1=lnb[:], op=ALU.add), reads=["lnb", ("xt", xt_slot)], writes=[("xt", xt_slot)])
                yield
            cx.dma("pool", dst[tt * TS:(tt + 1) * TS, :].rearrange("(j p) d -> p j d", p=128), xt[xt_slot][:], reads=[("xt", xt_slot)], writes=[("xs", tt)])
            yield

        def load_x(src, tt, xt, slot):
            cx.dma("sp", xt[slot][:], src[tt * TS:(tt + 1) * TS, :].rearrange("(j p) d -> p j d", p=128), reads=[("xs", tt)], writes=[("xt", slot)])

        def make_hT(xt, slot, psT, hT, sccols, shcols):
            for kc in range(NKC):
                pt = psT[kc % 2]
                for j in range(4):
                    cx.op("pe", lambda e, j=j: e.transpose(out=pt[:, j * 128:(j + 1) * 128], in_=xt[slot][:, j, kc * 128:(kc + 1) * 128], identity=ident), reads=[("xt", slot), "cf"], writes=[("psT", kc % 2)])
                cx.op("act", lambda e: e.activation(out=hT[:, kc, :], in_=pt[:], func=AF.Identity, bias=shcols[:, kc:kc + 1], scale=sccols[:, kc:kc + 1]), reads=[("psT", kc % 2), "mod"], writes=[("hT", kc)])

        def ffn(l, w, src, dst):
            j0 = 0 if w == 1 else 6
            shc = mod[:, l, j0 * 16:(j0 + 1) * 16]
            scc = mod[:, l, (j0 + 1) * 16:(j0 + 2) * 16]
            gc = mod[:, l, (j0 + 2) * 16:(j0 + 3) * 16]
            lni = 0 if w == 1 else 2
            with ExitStack() as fs:
                xt = [sb("xt%d" % i, [128, 4, D], F32, fs) for i in range(2)]
                hT = sb("hT", [128, NKC, TS], BF16, fs)
                gT = sb("gT", [128, NM, TS], BF16, fs)
                sg = [sb("sg%d" % i, [128, TS], F32, fs) for i in range(2)]
                yg = [sb("yg%d" % i, [128, TS], F32, fs) for i in range(2)]
                wa = [sb("wa%d" % i, [128, NKC, 256], BF16, fs) for i in range(2)]
                wo = [sb("wo%d" % i, [128, NM, 128], BF16, fs) for i in range(2)]
                lng = sb("lng", [128, D], F32, fs)
                lnb = sb("lnb", [128, D], F32, fs)
                st = sb("st", [128, 24], F32, fs)
                mv = sb("mv", [128, 2], F32, fs)
                rs = sb("rs", [128, 1], F32, fs)
                psT = [fs.enter_context(psum_tensor("psT%d" % i, [128, 512], F32)) for i in range(2)]
                psA = [fs.enter_context(psum_tensor("psA%d" % i, [128, 512], F32)) for i in range(4)]
                psY = [fs.enter_context(psum_tensor("psY%d" % i, [128, 512], F32)) for i in range(2)]
                cx.dma("sp", lng[:], ln_g[lni][l:l + 1, :].partition_broadcast(128), writes=["lng"])
                cx.dma("sp", lnb[:], ln_b[lni][l:l + 1, :].partition_broadcast(128), writes=["lnb"])
                stream = []
                for tt in range(NT):
                    stream += [("a", tt, m) for m in range(NM)] + [("b", tt, mo) for mo in range(NKC)]
                cnt = {"a": 0, "b": 0}
                slots = []
                for kind, tt, i in stream:
                    slots.append(cnt[kind] % 2)
                    cnt[kind] += 1

                def issue(si):
                    if si >= len(stream):
                        return
                    kind, tt, i = stream[si]
                    s_ = slots[si]
                    if kind == "a":
                        cx.dma("sp", wa[s_][:], wi_s[(l, w)][i], reads=[("wi", l, w, i)], writes=[("wa", s_)])
                    else:
                        cx.dma("sp", wo[s_][:], wo_s[(l, w)][i], reads=[("wo", l, w, i)], writes=[("wo", s_)])

                load_x(src, 0, xt, 0)
                issue(0)
                si = 0
                pend = [None, None]

                def adv(n):
                    for _ in range(n):
                        if pend[0] is not None:
                            try:
                                next(pend[0])
                            except StopIteration:
                                pend[0] = None
                        if pend[0] is None and pend[1] is not None:
                            load_x(src, pend[1], xt, pend[1] % 2)
                            pend[1] = None
                for tt in range(NT):
                    xs_ = tt % 2
                    if tt + 1 < NT:
                        pend[1] = tt + 1
                        if pend[0] is None:
                            adv(1)
                    make_hT(xt, xs_, psT, hT, scc, shc)
                    for m in range(NM):
                        issue(si + 1)
                        s_ = slots[si]
                        pa_, pb_ = psA[(m % 2) * 2], psA[(m % 2) * 2 + 1]
                        for k in range(NKC):
                            cx.op("pe", lambda e, k=k: e.matmul(pa_[:], lhsT=wa[s_][:, k, 0:128], rhs=hT[:, k, :], start=(k == 0), stop=(k == NKC - 1)),
                                  reads=[("wa", s_), ("hT", k)], writes=[("psA", (m % 2) * 2)])
                        for k in range(NKC):
                            cx.op("pe", lambda e, k=k: e.matmul(pb_[:], lhsT=wa[s_][:, k, 128:256], rhs=hT[:, k, :], start=(k == 0), stop=(k == NKC - 1)),
                                  reads=[("wa", s_), ("hT", k)], writes=[("psA", (m % 2) * 2 + 1)])
                        cx.op("act", lambda e: e.activation(out=sg[m % 2][:], in_=pa_[:], func=AF.Silu), reads=[("psA", (m % 2) * 2)], writes=[("sg", m % 2)])
                        cx.op("dve", lambda e: e.tensor_tensor(out=gT[:, m, :], in0=sg[m % 2][:], in1=pb_[:], op=ALU.mult), reads=[("sg", m % 2), ("psA", (m % 2) * 2 + 1)], writes=[("gT", m)])
                        si += 1
                        adv(4)
                        if m % 4 == 0:
                            tick()
                    adv(1000)
                    for mo in range(NKC):
                        issue(si + 1)
                        s_ = slots[si]
                        py = psY[mo % 2]
                        for k in range(NM):
                            cx.op("pe", lambda e, k=k: e.matmul(py[:], lhsT=wo[s_][:, k, :], rhs=gT[:, k, :], start=(k == 0), stop=(k == NM - 1)),
                                  reads=[("wo", s_), ("gT", k)], writes=[("psY", id(py))])
                        g_ = resid_ln_tail(l, lni, mo, py, gc, xs_, xt, yg, psT, lng, lnb, st, mv, rs, dst, tt)
                        if g_ is not None:
                            pend[0] = g_
                        si += 1
                        tick()
                adv(1000)
                cx.barrier()

        def mixer(l, src, dst):
            shc = mod[:, l, 48:64]
            scc = mod[:, l, 64:80]
            gc = mod[:, l, 80:96]
            with ExitStack() as fs:
                xt = [sb("xt%d" % i, [128, 4, D], F32, fs) for i in range(2)]
                hTb = [sb("hTb%d" % i, [128, NKC, TS], BF16, fs) for i in range(2)]
                psT = [fs.enter_context(psum_tensor("psT%d" % i, [128, 512], F32)) for i in range(2)]
                load_x(src, 0, xt, 0)
                for tt in range(NT):
                    if tt + 1 < NT:
                        load_x(src, tt + 1, xt, (tt + 1) % 2)
                    hT = hTb[tt % 2]
                    for kc in range(NKC):
                        pt = psT[kc % 2]
                        for j in range(4):
                            cx.op("pe", lambda e, j=j: e.transpose(out=pt[:, j * 128:(j + 1) * 128], in_=xt[tt % 2][:, j, kc * 128:(kc + 1) * 128], identity=ident), reads=[("xt", tt % 2), "cf"], writes=[("psT", kc % 2)])
                        cx.op("act", lambda e: e.activation(out=hT[:, kc, :], in_=pt[:], func=AF.Identity, bias=shc[:, kc:kc + 1], scale=scc[:, kc:kc + 1]), reads=[("psT", kc % 2), "mod"], writes=[("hTb", tt % 2)])
                    cx.dma("pool", hsc[:, :, tt * TS:(tt + 1) * TS], hT[:], reads=[("hTb", tt % 2)], writes=[("hsc", tt)])
                    tick(2)
                cx.barrier()

            def load_h(hTt, slot, tt):
                cx.dma("sp", hTt[slot][:], hsc[:, :, tt * TS:(tt + 1) * TS], reads=[("hsc", tt)], writes=[("hTt", slot)])

            def finalize(totv, key_tot, gcolv, gate, psR, osq, rt, yTt, u, tt, tmpk, gkey="gate", rkey="psR", lnexp=False):
                cx.op("act", lambda e: e.activation(out=osq[:], in_=totv, func=AF.Square), reads=[key_tot], writes=["osq"])
                yield
                cx.op("pe", lambda e: e.matmul(psR[:], lhsT=onesm, rhs=osq[:], start=True, stop=True), reads=["osq", "cf"], writes=[rkey])
                yield
                if lnexp:
                    cx.op("act", lambda e: e.activation(out=rt[:], in_=psR[:], func=AF.Ln, bias=rmseps, scale=1.0), reads=[rkey, "cf"], writes=["rt"])
                    yield
                    cx.op("act", lambda e: e.activation(out=rt[:], in_=rt[:], func=AF.Exp, scale=-0.5), reads=["rt"], writes=["rt"])
                    yield
                else:
                    cx.op("act", lambda e: e.activation(out=rt[:], in_=psR[:], func=AF.Sqrt, bias=rmseps, scale=1.0), reads=[rkey, "cf"], writes=["rt"])
                    yield
                    cx.op("dve", lambda e: e.reciprocal(out=rt[:], in_=rt[:]), reads=["rt"], writes=["rt"])
                    yield
                cx.op("dve", lambda e: e.tensor_tensor(out=rt[:], in0=rt[:], in1=totv, op=ALU.mult), reads=["rt", key_tot], writes=["rt"])
                yield
                ys = yTt[tmpk % 2]
                if gate is None:
                    cx.op("dve", lambda e: e.tensor_scalar(out=ys[:], in0=rt[:], scalar1=gcolv, scalar2=None, op0=ALU.mult), reads=["rt", "gcol"], writes=[("yTt", tmpk % 2)])
                    yield
                else:
                    cx.op("dve", lambda e: e.scalar_tensor_tensor(out=ys[:], in0=rt[:], scalar=gcolv, in1=gate, op0=ALU.mult, op1=ALU.mult), reads=["rt", "gcol", gkey], writes=[("yTt", tmpk % 2)])
                    yield
                cx.dma("pool", ysc[:, u, tt * TS:(tt + 1) * TS], ys[:], reads=[("yTt", tmpk % 2)], writes=[("ysc", u, tt)])
                yield

            with ExitStack() as fs:
                hTt = [sb("hTt%d" % i, [128, NKC, TS], BF16, fs) for i in range(2)]
                wu = sb("wuf", [128, NKC, 4, 128], BF16, fs)
                Wg = sb("Wg", [128, 4, S // 128, 256], BF16, fs)
                UTb = [sb("UTb%d" % i, [128, TS], BF16, fs) for i in range(2)]
                HS = S // 2
                csl = sb("csl", [128, S // 128, 256], BF16, fs)
                ssl = sb("ssl", [128, S // 128, 256], BF16, fs)
                osq = sb("osq", [128, TS], F32, fs)
                rt = sb("rt", [128, TS], F32, fs)
                ysb = sb("ysb", [128, TS], F32, fs)
                yTt = [sb("yTt%d" % i, [128, TS], BF16, fs) for i in range(2)]
                psP = [fs.enter_context(psum_tensor("psP%d" % i, [128, 512], F32)) for i in range(2)]
                psW = [fs.enter_context(psum_tensor("psW%d" % i, [128, 1024], F32)) for i in range(1)]
                psYf = [fs.enter_context(psum_tensor("psYf%d" % i, [128, 512], F32)) for i in range(2)]
                psR = fs.enter_context(psum_tensor("psR", [128, 512], F32))
                for g in range(4):
                    cx.dma("sp", wu[:, :, g, :], win_s[l][g], reads=[("win", l, g)], writes=["wu"])
                scl = 1.0 / float(np.sqrt(S * 128.0))
                load_h(hTt, 0, 0)
                for tt in range(NT):
                    if tt + 1 < NT:
                        load_h(hTt, (tt + 1) % 2, tt + 1)
                    for g in range(4):
                        ki = (tt * 4 + g)
                        pp_ = psP[ki % 2]
                        for k in range(NKC):
                            cx.op("pe", lambda e, k=k: e.matmul(pp_[:], lhsT=wu[:, k, g, :], rhs=hTt[tt % 2][:, k, :], start=(k == 0), stop=(k == NKC - 1)), reads=["wu", ("hTt", tt % 2)], writes=[("psP", ki % 2)])
                        ub = UTb[ki % 2]
                        cx.op("act", lambda e: e.activation(out=ub[:], in_=pp_[:], func=AF.Copy, scale=scl), reads=[("psP", ki % 2)], writes=[("UTb", ki % 2)])
                        pw = psW[0]
                        for j in range(4):
                            cx.op("pe", lambda e, j=j: e.matmul(pw[:, j * 256:(j + 1) * 256], lhsT=ub[:, j * 128:(j + 1) * 128], rhs=csg, start=True, stop=True), reads=[("UTb", ki % 2), "cb"], writes=["psW"])
                        cx.op("dve", lambda e: e.tensor_copy(out=Wg[:, g, tt * 4:(tt + 1) * 4, :], in_=pw[:].rearrange("p (j c) -> p j c", c=256)), reads=["psW"], writes=["Wg"])
                    tick(2)
                NB = S // 128
                PSL = 256
                fk = 0
                for psl in range(S // PSL):
                    cx.dma("sp", csl[:], dftc.rearrange("(b p) n -> p b n", p=128)[:, :, psl * PSL:(psl + 1) * PSL], writes=["csl"])
                    cx.dma("sp", ssl[:], dfts.rearrange("(b p) n -> p b n", p=128)[:, :, psl * PSL:(psl + 1) * PSL], writes=["ssl"])
                    for g in range(4):
                        py = psYf[fk % 2]
                        for b_ in range(NB):
                            cx.op("pe", lambda e, b_=b_: e.matmul(py[:, 0:PSL], lhsT=Wg[:, g, b_, 0:128], rhs=csl[:, b_, :], start=(b_ == 0), stop=False), reads=["Wg", "csl"], writes=[("psYf", fk % 2)])
                            cx.op("pe", lambda e, b_=b_: e.matmul(py[:, 0:PSL], lhsT=Wg[:, g, b_, 128:256], rhs=ssl[:, b_, :], start=False, stop=(b_ == NB - 1)), reads=["Wg", "ssl"], writes=[("psYf", fk % 2)])
                        cx.op("act", lambda e: e.activation(out=ysb[:, 0:PSL], in_=py[:, 0:PSL], func=AF.Copy), reads=[("psYf", fk % 2)], writes=["ysb"])
                        cx.op("act", lambda e: e.activation(out=osq[:, 0:PSL], in_=ysb[:, 0:PSL], func=AF.Square), reads=["ysb"], writes=["osq"])
                        cx.op("pe", lambda e: e.matmul(psR[:, 0:PSL], lhsT=onesm, rhs=osq[:, 0:PSL], start=True, stop=True), reads=["osq", "cf"], writes=["psR"])
                        cx.op("act", lambda e: e.activation(out=rt[:, 0:PSL], in_=psR[:, 0:PSL], func=AF.Sqrt, bias=rmseps, scale=1.0), reads=["psR", "cf"], writes=["rt"])
                        cx.op("dve", lambda e: e.reciprocal(out=rt[:, 0:PSL], in_=rt[:, 0:PSL]), reads=["rt"], writes=["rt"])
                        cx.op("dve", lambda e: e.tensor_tensor(out=rt[:, 0:PSL], in0=rt[:, 0:PSL], in1=ysb[:, 0:PSL], op=ALU.mult), reads=["rt", "ysb"], writes=["rt"])
                        ys = yTt[fk % 2]
                        cx.op("dve", lambda e: e.tensor_scalar(out=ys[:, 0:PSL], in0=rt[:, 0:PSL], scalar1=gcol[:, l * 4 + g:l * 4 + g + 1], scalar2=None, op0=ALU.mult), reads=["rt", "gcol"], writes=[("yTt", fk % 2)])
                        cx.dma("pool", ysc[:, g, psl * PSL:(psl + 1) * PSL], ys[:, 0:PSL], reads=[("yTt", fk % 2)], writes=[("ysc", g, psl)])
                        fk += 1
                    tick(2)
                cx.barrier()

            with ExitStack() as fs:
                hTt = [sb("hTt%d" % i, [128, NKC, TS], BF16, fs) for i in range(2)]
                wus = [sb("wuh%d" % i, [128, NKC, 5, 128], BF16, fs) for i in range(2)]
                qs = sb("qs", [128, S], BF16, fs)
                vtok = sb("vtok", [128, S // 128, 128], BF16, fs)
                gate = sb("gate", [128, S], BF16, fs)
                zb = sb("zb", [128, S], F32, fs)
                oacc = sb("oacc", [128, S], F32, fs)
                tmps = [[sb("tmp%d_%d" % (b_, i), [128, TS], F32, fs) for i in range(6)] for b_ in range(2)]
                vTt = sb("vTt", [128, TS], BF16, fs)
                qins = [sb("qin%d" % i, [128, TS], BF16, fs) for i in range(2)]
                kins = [sb("kin%d" % i, [128, TS], BF16, fs) for i in range(2)]
                qsts = [sb("qst%d" % i, [128, TS], BF16, fs) for i in range(2)]
                ksts = [sb("kst%d" % i, [128, TS], BF16, fs) for i in range(2)]
                Am = sb("Am", [128, TS], BF16, fs)
                ktok = sb("ktok", [128, TS], BF16, fs)
                Sbfs = [sb("Sbf%d" % i, [128, 8, 128], BF16, fs) for i in range(2)]
                Sr = sb("Sr", [128, 9, 128], F32, fs)
                osq = sb("osq", [128, TS], F32, fs)
                rt = sb("rt", [128, TS], F32, fs)
                totb = sb("totb", [128, TS], F32, fs)
                yTt = [sb("yTt%d" % i, [128, TS], BF16, fs) for i in range(2)]
                psP = [fs.enter_context(psum_tensor("psP%d" % i, [128, 512], F32)) for i in range(3)]
                psAA = fs.enter_context(psum_tensor("psAA", [128, 512], F32))
                psK = fs.enter_context(psum_tensor("psK", [128, 1024], BF16))
                psO = fs.enter_context(psum_tensor("psO", [128, 512], F32))
                psU = [fs.enter_context(psum_tensor("psU%d" % i, [128, 512], F32)) for i in range(2)]
                psR = psP[2]
                fcount = [0]

                def load_wu(h):
                    for ci in range(5):
                        cc = 4 + ci * NH + h
                        cx.dma("sp", wus[h % 2][:, :, ci, :], win_s[l][cc], reads=[("win", l, cc)], writes=[("wu", h % 2)])

                def P(h, tt):
                    wu = wus[h % 2]
                    bs = tt % 2
                    if tt + 1 < NT:
                        load_h(hTt, (tt + 1) % 2, tt + 1)
                    tsl = slice(tt * TS, (tt + 1) * TS)

                    def proj(ci, bank):
                        for k in range(NKC):
                            cx.op("pe", lambda e, k=k: e.matmul(psP[bank][:], lhsT=wu[:, k, ci, :], rhs=hTt[tt % 2][:, k, :], start=(k == 0), stop=(k == NKC - 1)),
                                  reads=[("wu", h % 2), ("hTt", tt % 2)], writes=[("psP", bank)])
                            yield
                    yield from proj(0, 0)
                    yield from proj(4, 1)
                    yield from proj(2, 2)
                    cx.op("act", lambda e: e.activation(out=qs[:, tsl], in_=psP[0][:], func=AF.Silu), reads=[("psP", 0)], writes=[("qs", tt)])
                    yield
                    cx.op("act", lambda e: e.activation(out=gate[:, tsl], in_=psP[1][:], func=AF.Silu), reads=[("psP", 1)], writes=[("gate", tt)])
                    yield
                    cx.op("act", lambda e: e.activation(out=tmps[bs][0][:], in_=psP[2][:], func=AF.Exp, scale=-1.0), reads=[("psP", 2)], writes=[("t1", bs)])
                    yield
                    yield from proj(1, 0)
                    yield from proj(3, 1)
                    cx.op("act", lambda e: e.activation(out=vTt[:], in_=psP[0][:], func=AF.Copy), reads=[("psP", 0)], writes=["vTt"])
                    yield
                    cx.op("dve", lambda e: e.tensor_copy(out=zb[:, tsl], in_=psP[1][:]), reads=[("psP", 1)], writes=[("zb", tt)])
                    yield
                    for j in range(4):
                        cx.op("pe", lambda e, j=j: e.transpose(out=psK[:, 512 + j * 128:512 + (j + 1) * 128], in_=vTt[:, j * 128:(j + 1) * 128], identity=identb), reads=["vTt", "cb"], writes=["psK"])
                        yield
                    cx.op("dve", lambda e: e.tensor_copy(out=vtok[:, tt * 4:(tt + 1) * 4, :], in_=psK[:, 512:1024].rearrange("p (j c) -> p j c", c=128)), reads=["psK"], writes=[("vtok", tt)])
                    yield

                def Fr(h, tt, dr, bs):
                    lcol = lbc[:, l, dr * 12 + h:dr * 12 + h + 1]
                    ocol = omlc[:, l, dr * 12 + h:dr * 12 + h + 1]
                    t1, t2, t3, t4, t5, t6 = tmps[bs]
                    K1, K2, K3, K4, K5, K6 = [("t%d" % (i + 1), bs) for i in range(6)]
                    qin, kin, qst, kst = qins[bs], kins[bs], qsts[bs], ksts[bs]
                    tsl = slice(tt * TS, (tt + 1) * TS)
                    qv = qs[:, tsl]
                    QK = ("qs", tt)
                    if dr == 1:
                        cx.op("act", lambda e: e.activation(out=t1[:], in_=zb[:, tsl], func=AF.Exp, scale=-1.0), reads=[("zb", tt)], writes=[K1])
                        yield
                    cx.op("act", lambda e: e.activation(out=t1[:], in_=t1[:], func=AF.Ln, bias=cf[:, 768:769], scale=1.0), reads=[K1, "cf"], writes=[K1])
                    yield
                    cx.op("act", lambda e: e.activation(out=t1[:], in_=t1[:], func=AF.Exp, scale=-1.0), reads=[K1], writes=[K1])
                    yield
                    cx.op("dve", lambda e: e.tensor_scalar(out=t1[:], in0=t1[:], scalar1=ocol, scalar2=lcol, op0=ALU.mult, op1=ALU.add), reads=[K1, "lbc"], writes=[K1])
                    yield
                    cx.op("act", lambda e: e.activation(out=t2[:], in_=t1[:], func=AF.Ln), reads=[K1], writes=[K2])
                    yield
                    cx.op("dve", lambda e: e.tensor_scalar(out=t1[:], in0=t1[:], scalar1=-1.0, scalar2=1.0, op0=ALU.mult, op1=ALU.add), reads=[K1], writes=[K1])
                    yield
                    cx.op("dve", lambda e: e.tensor_tensor_scan(out=t3[:], data0=rmask, data1=t2[:], initial=0.0, op0=ALU.mult, op1=ALU.add), reads=[K2, "cf"], writes=[K3])
                    yield
                    if dr == 0:
                        cbuf, ckey, r = t3, K3, 32
                    else:
                        cx.op("dve", lambda e: e.tensor_tensor(out=t2[:], in0=t3[:], in1=t2[:], op=ALU.subtract), reads=[K2, K3], writes=[K2])
                        yield
                        cbuf, ckey, r = t2, K2, 31
                    c3 = cbuf[:].rearrange("p (n c) -> p n c", c=CH)
                    cum3 = t3[:].rearrange("p (n c) -> p n c", c=CH)

                    def v3(tb):
                        return tb[:].rearrange("p (n c) -> p n c", c=CH)
                    cx.op("dve", lambda e: e.tensor_tensor(out=v3(t4), in0=c3, in1=c3[:, :, r:r + 1].to_broadcast([128, 8, CH]), op=ALU.subtract), reads=[ckey], writes=[K4])
                    yield
                    sg_ = 1.0 if dr == 0 else -1.0
                    cx.op("act", lambda e: e.activation(out=t5[:], in_=t4[:], func=AF.Exp, scale=sg_), reads=[K4], writes=[K5])
                    yield
                    cx.op("act", lambda e: e.activation(out=t4[:], in_=t4[:], func=AF.Exp, scale=-sg_), reads=[K4], writes=[K4])
                    yield
                    cx.op("dve", lambda e: e.tensor_tensor(out=qin[:], in0=qv, in1=t5[:], op=ALU.mult), reads=[QK, K5], writes=[("qin", bs)])
                    yield
                    cx.op("dve", lambda e: e.tensor_tensor(out=kin[:], in0=t1[:], in1=t4[:], op=ALU.mult), reads=[K1, K4], writes=[("kin", bs)])
                    yield
                    if dr == 0:
                        cx.op("dve", lambda e: e.tensor_tensor(out=v3(t6), in0=c3, in1=c3[:, :, CH - 1:CH].to_broadcast([128, 8, CH]), op=ALU.subtract), reads=[K3], writes=[K6])
                        yield
                        cx.op("act", lambda e: e.activation(out=t3[:], in_=t3[:], func=AF.Exp), reads=[K3], writes=[K3])
                        yield
                        cx.op("act", lambda e: e.activation(out=t6[:], in_=t6[:], func=AF.Exp, scale=-1.0), reads=[K6], writes=[K6])
                        yield
                        cx.op("dve", lambda e: e.tensor_tensor(out=qst[:], in0=qv, in1=t3[:], op=ALU.mult), reads=[QK, K3], writes=[("qst", bs)])
                        yield
                        cx.op("dve", lambda e: e.tensor_tensor(out=kst[:], in0=t1[:], in1=t6[:], op=ALU.mult), reads=[K1, K6], writes=[("kst", bs)])
                        yield
                    else:
                        cx.op("dve", lambda e: e.tensor_tensor(out=v3(t6), in0=c3, in1=cum3[:, :, CH - 1:CH].to_broadcast([128, 8, CH]), op=ALU.subtract), reads=[K2, K3], writes=[K6])
                        yield
                        cx.op("act", lambda e: e.activation(out=t6[:], in_=t6[:], func=AF.Exp, scale=-1.0), reads=[K6], writes=[K6])
                        yield
                        cx.op("act", lambda e: e.activation(out=t2[:], in_=t2[:], func=AF.Exp), reads=[K2], writes=[K2])
                        yield
                        cx.op("act", lambda e: e.activation(out=t3[:], in_=t3[:], func=AF.Exp), reads=[K3], writes=[K3])
                        yield
                        cx.op("dve", lambda e: e.tensor_tensor(out=qst[:], in0=qv, in1=t6[:], op=ALU.mult), reads=[QK, K6], writes=[("qst", bs)])
                        yield
                        cx.op("dve", lambda e: e.tensor_tensor(out=kst[:], in0=t1[:], in1=t2[:], op=ALU.mult), reads=[K1, K2], writes=[("kst", bs)])
                        yield

                def B1(h, tt, dr, bs):
                    t3 = tmps[bs][2]
                    K3 = ("t3", bs)
                    qin, kin, qst, kst = qins[bs], kins[bs], qsts[bs], ksts[bs]
                    Sbf = Sbfs[bs]
                    d3 = t3[:].rearrange("p (n c) -> p n c", c=CH)
                    VK = ("vtok", tt)
                    for j in range(4):
                        cx.op("pe", lambda e, j=j: e.matmul(psAA[:, j * 128:(j + 1) * 128], lhsT=kin[:, j * 128:(j + 1) * 128], rhs=qin[:, j * 128:(j + 1) * 128], start=True, stop=True), reads=[("kin", bs), ("qin", bs)], writes=["psAA"])
                        yield
                    mk = maskF if dr == 0 else maskB
                    cx.op("dve", lambda e: e.tensor_tensor(out=Am[:], in0=psAA[:], in1=mk, op=ALU.mult), reads=["psAA", "cb"], writes=["Am"])
                    yield
                    for j in range(4):
                        cx.op("pe", lambda e, j=j: e.transpose(out=psK[:, j * 128:(j + 1) * 128], in_=kst[:, j * 128:(j + 1) * 128], identity=identb), reads=[("kst", bs), "cb"], writes=["psK"])
                        yield
                    cx.op("act", lambda e: e.activation(out=ktok[:], in_=psK[:, 0:512], func=AF.Copy), reads=["psK"], writes=["ktok"])
                    yield
                    for hf in range(2):
                        for j in range(4):
                            pu = psU[hf]
                            cx.op("pe", lambda e, j=j, hf=hf, pu=pu: e.matmul(pu[:, j * 128:(j + 1) * 128], lhsT=ktok[hf * 64:(hf + 1) * 64, j * 128:(j + 1) * 128], rhs=vtok[hf * 64:(hf + 1) * 64, tt * 4 + j, :], start=True, stop=True),
                                  reads=["ktok", VK], writes=[("psU", hf)])
                            yield

                def CHN(h, tt, dr, bs):
                    t3 = tmps[bs][2]
                    K3 = ("t3", bs)
                    Sbf = Sbfs[bs]
                    d3 = t3[:].rearrange("p (n c) -> p n c", c=CH)
                    order = list(range(8)) if dr == 0 else list(range(7, -1, -1))
                    for i, n in enumerate(order):
                        pu = psU[n % 2]
                        cx.op("act", lambda e, n=n, i=i: e.activation(out=Sbf[:, n, :], in_=Sr[:, i, :], func=AF.Copy), reads=[("Sr", i)], writes=[("Sbf", bs, n)])
                        yield
                        cx.op("dve", lambda e, n=n, i=i, pu=pu: e.scalar_tensor_tensor(out=Sr[:, i + 1, :], in0=Sr[:, i, :], scalar=d3[:, n, CH - 1:CH], in1=pu[:, (n // 2) * 128:(n // 2 + 1) * 128], op0=ALU.mult, op1=ALU.add),
                              reads=[("Sr", i), K3, ("psU", n % 2)], writes=[("Sr", i + 1)])
                        yield
                    cx.op("dve", lambda e: e.tensor_copy(out=Sr[:, 0, :], in_=Sr[:, 8, :]), reads=[("Sr", 8)], writes=[("Sr", 0)])
                    yield

                def B2(h, tt, dr, bs):
                    qst = qsts[bs]
                    Sbf = Sbfs[bs]
                    VK = ("vtok", tt)
                    for j in range(4):
                        cx.op("pe", lambda e, j=j: e.matmul(psO[:, j * 128:(j + 1) * 128], lhsT=vtok[:, tt * 4 + j, :], rhs=Am[:, j * 128:(j + 1) * 128], start=True, stop=False), reads=[VK, "Am"], writes=["psO"])
                        yield
                        for hf in range(2):
                            n = 2 * j + hf
                            cx.op("pe", lambda e, n=n, hf=hf: e.matmul(psO[:, n * 64:(n + 1) * 64], lhsT=Sbf[:, n, :], rhs=qst[:, n * 64:(n + 1) * 64], start=False, stop=(hf == 1)), reads=[("Sbf", bs, n), ("qst", bs)], writes=["psO"])
                            yield

                def run(*gens):
                    gens = [g for g in gens if g is not None]
                    while gens:
                        for g in list(gens):
                            try:
                                next(g)
                            except StopIteration:
                                gens.remove(g)

                def seq(*gens):
                    for g in gens:
                        yield from g

                def evac_f(tt):
                    tsl = slice(tt * TS, (tt + 1) * TS)
                    cx.op("act", lambda e: e.activation(out=oacc[:, tsl], in_=psO[:], func=AF.Copy), reads=["psO"], writes=[("oacc", tt)])
                    yield

                def evac_b(h, tt):
                    tsl = slice(tt * TS, (tt + 1) * TS)
                    cx.op("dve", lambda e: e.tensor_tensor(out=totb[:], in0=oacc[:, tsl], in1=psO[:], op=ALU.add), reads=[("oacc", tt), "psO"], writes=["totb"])
                    yield
                    yield from finalize(totb[:], "totb", gcol[:, 16 + l * NH + h:16 + l * NH + h + 1], gate[:, tsl], psR, osq, rt, yTt, 4 + h, tt, fcount[0], gkey=("gate", tt), rkey=("psP", 2), lnexp=True)
                    fcount[0] += 1

                load_wu(0)
                for h in range(NH):
                    if h + 1 < NH:
                        load_wu(h + 1)
                    cx.op("dve", lambda e: e.memset(Sr[:, 0, :], 0.0), writes=[("Sr", 0)])
                    load_h(hTt, 0, 0)
                    run(P(h, 0))
                    run(Fr(h, 0, 0, 0), P(h, 1) if NT > 1 else None)
                    for tt in range(NT):
                        bs = tt % 2
                        run(seq(B1(h, tt, 0, bs), CHN(h, tt, 0, bs), B2(h, tt, 0, bs), evac_f(tt)),
                            Fr(h, tt + 1, 0, (tt + 1) % 2) if tt + 1 < NT else None,
                            P(h, tt + 2) if tt + 2 < NT else None)
                        tick(1)
                    cx.op("dve", lambda e: e.memset(Sr[:, 0, :], 0.0), writes=[("Sr", 0)])
                    order_t = list(range(NT - 1, -1, -1))
                    run(Fr(h, order_t[0], 1, 0))
                    for idx, tt in enumerate(order_t):
                        bs = idx % 2
                        run(seq(B1(h, tt, 1, bs), CHN(h, tt, 1, bs), B2(h, tt, 1, bs), evac_b(h, tt)),
                            Fr(h, order_t[idx + 1], 1, (idx + 1) % 2) if idx + 1 < NT else None)
                        tick(1)
                cx.barrier()

            with ExitStack() as fs:
                xt = [sb("xt%d" % i, [128, 4, D], F32, fs) for i in range(2)]
                yTl = [sb("yTl%d" % i, [128, NU, TS], BF16, fs) for i in range(2)]
                wob = sb("wob", [128, NKC, D], BF16, fs)
                yg = [sb("yg%d" % i, [128, TS], F32, fs) for i in range(2)]
                lng = sb("lng", [128, D], F32, fs)
                lnb = sb("lnb", [128, D], F32, fs)
                st = sb("st", [128, 24], F32, fs)
                mv = sb("mv", [128, 2], F32, fs)
                rs = sb("rs", [128, 1], F32, fs)
                psT = [fs.enter_context(psum_tensor("psT%d" % i, [128, 512], F32)) for i in range(2)]
                psY = [fs.enter_context(psum_tensor("psY%d" % i, [128, 512], F32)) for i in range(2)]
                cx.dma("sp", lng[:], ln_g[1][l:l + 1, :].partition_broadcast(128), writes=["lng"])
                cx.dma("sp", lnb[:], ln_b[1][l:l + 1, :].partition_broadcast(128), writes=["lnb"])
                for k4 in range(4):
                    cx.dma("sp", wob[:, k4 * 4:(k4 + 1) * 4, :], wout_s[l][:, k4 * 4:(k4 + 1) * 4, :], reads=[("wout", l, k4)], writes=["wob"])

                def load_y(tt):
                    cx.dma("sp", yTl[tt % 2][:], ysc[:, :, tt * TS:(tt + 1) * TS], reads=[("ysc", u_, p_) for u_ in range(NU) for p_ in range(max(NT, S // 256))], writes=[("yTl", tt % 2)])
                load_x(src, 0, xt, 0)
                load_y(0)
                pend = [None, None]

                def adv(n):
                    for _ in range(n):
                        if pend[0] is not None:
                            try:
                                next(pend[0])
                            except StopIteration:
                                pend[0] = None
                        if pend[0] is None and pend[1] is not None:
                            load_x(src, pend[1], xt, pend[1] % 2)
                            pend[1] = None
                for tt in range(NT):
                    if tt + 1 < NT:
                        pend[1] = tt + 1
                        if pend[0] is None:
                            adv(1)
                        load_y(tt + 1)
                    for mo in range(NKC):
                        if mo == NKC - 1:
                            adv(1000)
                        py = psY[mo % 2]
                        for k in range(NU):
                            cx.op("pe", lambda e, k=k: e.matmul(py[:], lhsT=wob[:, k, mo * 128:(mo + 1) * 128], rhs=yTl[tt % 2][:, k, :], start=(k == 0), stop=(k == NU - 1)), reads=["wob", ("yTl", tt % 2)], writes=[("psY", id(py))])
                        g_ = resid_ln_tail(l, 1, mo, py, gc, tt % 2, xt, yg, psT, lng, lnb, st, mv, rs, dst, tt)
                        if g_ is not None:
                            pend[0] = g_
                        else:
                            adv(6)
                    tick(4)
                adv(1000)
                cx.barrier()

        nplan = len(plan)
        for pi, p in enumerate(plan):
            for f in cast_list(p[0], p[1], p[2] if p[0] == "ffn" else None):
                pending_casts.append((pi, f))
        for pi, p in enumerate(plan):
            src = x_in if pi == 0 else xs
            dst = y_out if pi == nplan - 1 else xs
            while pending_casts and pending_casts[0][0] <= pi:
                pending_casts.pop(0)[1]()
            if p[0] == "ffn":
                ffn(p[1], p[2], src, dst)
            else:
                mixer(p[1], src, dst)
        while pending_casts:
            tick()
        cx.finish()
    return nc


_CACHE = {}


def kernel(**inputs):
    S = inputs["x"].shape[1]
    B = inputs["x"].shape[0]
    plan = []
    for l in range(DEPTH):
        plan += [("ffn", l, 1), ("mix", l), ("ffn", l, 3)]
    key = (S, tuple(plan))
    if key not in _CACHE:
        _CACHE[key] = build(S, plan, list(range(DEPTH)))
    nc = _CACHE[key]
    cf, cb, dc, ds = _consts(S)
    in_maps = []
    for b in range(B):
        m = {k: np.ascontiguousarray(v) for k, v in inputs.items() if k not in ("x", "c")}
        m["x"] = np.ascontiguousarray(inputs["x"][b])
        m["c"] = np.ascontiguousarray(inputs["c"][b:b + 1])
        m["cst_f"] = cf
        m["cst_b"] = cb
        m["dft_c"] = dc
        m["dft_s"] = ds
        in_maps.append(m)
    res = run_bass_kernel_spmd(nc, in_maps, core_ids=list(range(B)))
    return np.stack([np.asarray(r["y"]) for r in res.results], axis=0).astype(np.float32)
```

```python
import numpy as np
import ml_dtypes
from contextlib import ExitStack
import concourse.bass as bass
import concourse.mybir as mybir
from concourse.bass_utils import run_bass_kernel_spmd

F32 = mybir.dt.float32
BF16 = mybir.dt.bfloat16
ALU = mybir.AluOpType
AF = mybir.ActivationFunctionType

D = 2048
NKC = 16
DFF = 5504
NM = 43
DIN = 8192
DEPTH = 4
NH = 12
NU = 16
CH = 64
TS = 512
ALPHA = (2 * DEPTH) ** 0.25
LN_EPS = 1e-5
RMS_EPS = 1e-6


class CX:
    def __init__(self, nc):
        self.nc = nc
        self.eng = {"pe": nc.tensor, "act": nc.scalar, "dve": nc.vector, "pool": nc.gpsimd, "sp": nc.sync}
        self.sem = {}
        self.cnt = {}
        for e in self.eng:
            self.sem[e] = nc.semaphore("sem_" + e).__enter__()
            self.cnt[e] = 0
        self.KQ = 12
        self.dq = {q: [nc.semaphore("dq_%s_%d" % (q, i)).__enter__() for i in range(self.KQ)] for q in ("sp", "pool")}
        self.dqi = {"sp": 0, "pool": 0}
        self.waited = {e: {} for e in self.eng}
        self.lastw = {}
        self.readers = {}
        self.allsems = {}

    def _wait(self, e, tok):
        key, sem, val, src = tok
        if self.waited[e].get(key, 0) >= val:
            return
        self.eng[e].wait_ge(sem, val)
        self.waited[e][key] = val

    def _deps(self, e, reads, writes):
        toks = []
        for k in list(reads) + list(writes):
            t = self.lastw.get(k)
            if t is not None:
                toks.append(t)
        for k in writes:
            r = self.readers.get(k)
            if r:
                toks.extend(r.values())
        for t in toks:
            if t[3] == "pe" and e == "pe":
                continue
            self._wait(e, t)

    def _record(self, tok, reads, writes):
        for k in writes:
            self.lastw[k] = tok
            self.readers[k] = {}
        for k in reads:
            d = self.readers.setdefault(k, {})
            d[tok[0]] = tok
        self.allsems[tok[0]] = tok

    def op(self, e, fn, reads=(), writes=()):
        self._deps(e, reads, writes)
        ins = fn(self.eng[e])
        self.cnt[e] += 1
        ins.then_inc(self.sem[e], 1)
        tok = ("E" + e, self.sem[e], self.cnt[e], e)
        self._record(tok, reads, writes)
        return tok

    def dma(self, q, out, in_, reads=(), writes=()):
        i = self.dqi[q]
        self.dqi[q] += 1
        K = self.KQ
        sem = self.dq[q][i % K]
        key = "Q%s%d" % (q, i % K)
        if i >= K:
            self._wait(q, (key, sem, 16 * (i // K), "dma"))
        self._deps(q, reads, writes)
        ins = self.eng[q].dma_start(out=out, in_=in_)
        ins.then_inc(sem, 16)
        tok = (key, sem, 16 * (i // K + 1), "dma")
        self._record(tok, reads, writes)
        return tok

    def barrier(self):
        toks = list(self.allsems.values())
        for e in self.eng:
            for t in toks:
                self._wait(e, t)
        self.lastw = {}
        self.readers = {}

    def finish(self):
        for t in list(self.allsems.values()):
            self._wait("sp", t)


def _consts(S):
    cf = np.zeros((128, 128 + 512 + 128 + 8), np.float32)
    cf[:, 0:128] = np.eye(128, dtype=np.float32)
    rm = np.ones((512,), np.float32)
    rm[::CH] = 0.0
    cf[:, 128:640] = rm[None, :]
    cf[:, 640:768] = 1.0 / 128.0
    cf[:, 768] = 1.0
    cf[:, 769] = LN_EPS
    cf[:, 770] = RMS_EPS
    s = np.arange(128)[:, None]
    t = np.arange(128)[None, :]
    same = (s // CH) == (t // CH)
    mF = (same & (s <= t)).astype(np.float32)
    mB = (same & (s >= t)).astype(np.float32)
    cb = np.zeros((128, 512 + 512 + 256 + 128), np.float32)
    cb[:, 0:512] = np.tile(mF, (1, 4))
    cb[:, 512:1024] = np.tile(mB, (1, 4))
    dd = np.arange(128)
    ang = 2.0 * np.pi * ((dd[:, None] * dd[None, :]) % 128) / 128.0
    cb[:, 1024:1152] = np.cos(ang)
    cb[:, 1152:1280] = -np.sin(ang)
    cb[:, 1280:1408] = np.eye(128)
    tt = np.arange(S, dtype=np.int64)
    angs = 2.0 * np.pi * ((tt[:, None] * tt[None, :]) % S) / S
    dc = np.cos(angs).astype(ml_dtypes.bfloat16)
    ds = np.sin(angs).astype(ml_dtypes.bfloat16)
    return cf, cb.astype(ml_dtypes.bfloat16), dc, ds


def build(S, plan, layers_ada):
    NT = S // TS
    nc = bass.Bass("TRN2", target_bir_lowering=False)
    cx = CX(nc)

    def din(name, shape, dt=F32):
        return nc.dram_tensor(name, shape, dt, kind="ExternalInput").ap()

    x_in = din("x", [S, D])
    c_in = din("c", [1, D])
    w_ada = din("w_ada", [DEPTH, D, 9 * D])
    b_ada = din("b_ada", [DEPTH, 9 * D])
    w_f1i = din("w_ffn1_in", [DEPTH, D, 2 * DFF])
    w_f1o = din("w_ffn1_out", [DEPTH, DFF, D])
    ln_g = [din("ln1_g", [DEPTH, D]), din("ln2_g", [DEPTH, D]), din("ln3_g", [DEPTH, D])]
    ln_b = [din("ln1_b", [DEPTH, D]), din("ln2_b", [DEPTH, D]), din("ln3_b", [DEPTH, D])]
    w_in = din("w_in", [DEPTH, D, DIN])
    lowb = din("lower_bounds", [DEPTH, 2, NH * 128])
    fou_g = din("fourier_g", [DEPTH, 512])
    hg_g = din("hgrn_g", [DEPTH, NH * 128])
    w_out = din("w_out", [DEPTH, D, D])
    w_f2i = din("w_ffn2_in", [DEPTH, D, 2 * DFF])
    w_f2o = din("w_ffn2_out", [DEPTH, DFF, D])
    cf_in = din("cst_f", [128, 776])
    cb_in = din("cst_b", [128, 1408], BF16)
    dftc = din("dft_c", [S, S], BF16)
    dfts = din("dft_s", [S, S], BF16)
    y_out = nc.dram_tensor("y", [S, D], F32, kind="ExternalOutput").ap()

    def dscr(name, shape, dt=BF16):
        return nc.dram_tensor(name, shape, dt, kind="Internal").ap()

    xs = dscr("xs", [S, D], F32)
    hsc = dscr("hsc", [128, NKC, S])
    ysc = dscr("ysc", [128, NU, S])
    used_layers = sorted(set(p[1] for p in plan))
    wi_s = {}
    wo_s = {}
    for l in used_layers:
        for w in (1, 3):
            if ("ffn", l, w) in plan:
                wi_s[(l, w)] = dscr("wi_s%d_%d" % (l, w), [NM, 128, NKC, 256])
                wo_s[(l, w)] = dscr("wo_s%d_%d" % (l, w), [NKC, 128, NM, 128])
    win_s = {l: dscr("win_s%d" % l, [64, 128, NKC, 128]) for l in used_layers if ("mix", l) in plan}
    wout_s = {l: dscr("wout_s%d" % l, [128, NKC, D]) for l in used_layers if ("mix", l) in plan}

    def cast_list(kind, l, w=None):
        lst = []
        if kind == "ffn":
            wi = (w_f1i if w == 1 else w_f2i)[l].rearrange("(k p) n -> p k n", p=128)
            wo = (w_f1o if w == 1 else w_f2o)[l].rearrange("(k p) n -> p k n", p=128)
            for m in range(NM):
                def f(m=m):
                    cx.dma("pool", wi_s[(l, w)][m][:, :, 0:128], wi[:, :, m * 128:(m + 1) * 128], writes=[("wi", l, w, m)])
                    cx.dma("pool", wi_s[(l, w)][m][:, :, 128:256], wi[:, :, DFF + m * 128:DFF + (m + 1) * 128], writes=[("wi", l, w, m)])
                lst.append(f)
            for mo in range(NKC):
                def f(mo=mo):
                    cx.dma("pool", wo_s[(l, w)][mo], wo[:, :, mo * 128:(mo + 1) * 128], writes=[("wo", l, w, mo)])
                lst.append(f)
        else:
            wi = w_in[l].rearrange("(k p) n -> p k n", p=128)
            for cc in range(64):
                def f(cc=cc):
                    cx.dma("pool", win_s[l][cc], wi[:, :, cc * 128:(cc + 1) * 128], writes=[("win", l, cc)])
                lst.append(f)
            wo = w_out[l].rearrange("(k p) n -> p k n", p=128)
            for k4 in range(4):
                def f(k4=k4):
                    for c4 in range(4):
                        cx.dma("pool", wout_s[l][:, k4 * 4:(k4 + 1) * 4, c4 * 512:(c4 + 1) * 512], wo[:, k4 * 4:(k4 + 1) * 4, c4 * 512:(c4 + 1) * 512], writes=[("wout", l, k4)])
                lst.append(f)
        return lst

    pending_casts = []

    def tick(n=1):
        for _ in range(n):
            if pending_casts:
                pending_casts.pop(0)[1]()

    uid = [0]

    def psum_tensor(name, shape, dt):
        uid[0] += 1
        return nc.psum_tensor("%s_%d" % (name, uid[0]), shape, dt)

    with ExitStack() as gs:
        def sb(name, shape, dt=F32, stack=gs):
            uid[0] += 1
            return stack.enter_context(nc.sbuf_tensor("%s_%d" % (name, uid[0]), shape, dt))

        cf = sb("cf", [128, 776])
        cb = sb("cb", [128, 1408], BF16)
        ident = cf[:, 0:128]
        rmask = cf[:, 128:640]
        onesm = cf[:, 640:768]
        one11 = cf[0:1, 768:769]
        lneps = cf[:, 769:770]
        rmseps = cf[:, 770:771]
        maskF = cb[:, 0:512]
        maskB = cb[:, 512:1024]
        csg = cb[:, 1024:1280]
        identb = cb[:, 1280:1408]
        mod = sb("mod", [128, DEPTH, 144])
        lbc = sb("lbc", [128, DEPTH, 24])
        omlc = sb("omlc", [128, DEPTH, 24])
        gcol = sb("gcol", [128, 64])
        cx.dma("sp", cf[:], cf_in, writes=["cf"])
        cx.dma("sp", cb[:], cb_in, writes=["cb"])

        with ExitStack() as ps:
            cact = sb("cact", [128, NKC], F32, ps)
            crow = sb("crow", [16, 128], F32, ps)
            lbr = sb("lbr", [96, 128], F32, ps)
            gr = sb("gr", [64, 128], F32, ps)
            ex = sb("lbex", [128, 96], F32, ps)
            tot = sb("lbtot", [128, 24], F32, ps)
            wst = [sb("wst%d" % i, [128, NKC, 512], F32, ps) for i in range(2)]
            bst = [sb("bst%d" % i, [1, 512], F32, ps) for i in range(2)]
            pp = ps.enter_context(psum_tensor("pp", [128, 512], F32))
            pa = ps.enter_context(psum_tensor("pa", [128, 512], F32))
            cx.dma("sp", crow[:], c_in[0].rearrange("(j p) -> j p", p=128), writes=["crow"])
            cx.dma("sp", lbr[:], lowb.rearrange("l d (h k) -> (l d h) k", k=128), writes=["lbr"])
            cx.dma("sp", gr[0:16, :], fou_g.rearrange("l (g k) -> (l g) k", k=128), writes=["gr"])
            cx.dma("sp", gr[16:64, :], hg_g.rearrange("l (h k) -> (l h) k", k=128), writes=["gr"])
            cx.op("pe", lambda e: e.transpose(out=pp[:, 0:16], in_=crow[:], identity=ident[0:16, 0:16]), reads=["crow", "cf"], writes=["pp"])
            cx.op("act", lambda e: e.activation(out=cact[:], in_=pp[:, 0:16], func=AF.Silu), reads=["pp"], writes=["cact"])
            cx.op("pe", lambda e: e.transpose(out=pp[:, 0:96], in_=lbr[:], identity=ident[0:96, 0:96]), reads=["lbr", "cf"], writes=["pp"])
            cx.op("act", lambda e: e.activation(out=ex[:], in_=pp[:, 0:96], func=AF.Exp), reads=["pp"], writes=["ex"])
            cx.op("pe", lambda e: e.transpose(out=pp[:, 0:64], in_=gr[:], identity=ident[0:64, 0:64]), reads=["gr", "cf"], writes=["pp"])
            cx.op("dve", lambda e: e.tensor_copy(out=gcol[:], in_=pp[:, 0:64]), reads=["pp"], writes=["gcol"])
            cx.op("dve", lambda e: e.tensor_tensor(out=tot[:], in0=ex[:, 0:24], in1=ex[:, 24:48], op=ALU.add), reads=["ex"], writes=["tot"])
            cx.op("dve", lambda e: e.tensor_tensor(out=tot[:], in0=tot[:], in1=ex[:, 48:72], op=ALU.add), reads=["ex", "tot"], writes=["tot"])
            cx.op("dve", lambda e: e.tensor_tensor(out=tot[:], in0=tot[:], in1=ex[:, 72:96], op=ALU.add), reads=["ex", "tot"], writes=["tot"])
            cx.op("dve", lambda e: e.reciprocal(out=tot[:], in_=tot[:]), reads=["tot"], writes=["tot"])
            for l in range(DEPTH):
                cx.op("dve", lambda e, l=l: e.tensor_tensor(out=ex[:, l * 24:(l + 1) * 24], in0=ex[:, l * 24:(l + 1) * 24], in1=tot[:], op=ALU.mult), reads=["ex", "tot"], writes=["ex"])
            cx.op("dve", lambda e: e.memset(lbc[:, 0, :], 0.0), writes=["lbc"])
            for l in range(1, DEPTH):
                cx.op("dve", lambda e, l=l: e.tensor_tensor(out=lbc[:, l, :], in0=lbc[:, l - 1, :], in1=ex[:, l * 24:(l + 1) * 24], op=ALU.add), reads=["ex", "lbc"], writes=["lbc"])
            cx.op("dve", lambda e: e.tensor_scalar(out=omlc[:], in0=lbc[:], scalar1=-1.0, scalar2=1.0, op0=ALU.mult, op1=ALU.add), reads=["lbc"], writes=["omlc"])
            adarow = [sb("adarow%d" % i, [1, 512], F32, ps) for i in range(2)]
            prow = [ps.enter_context(psum_tensor("prow%d" % i, [128, 512], F32)) for i in range(2)]
            for l in layers_ada:
                wv = w_ada[l].rearrange("(k p) n -> p k n", p=128)
                for sl in range(36):
                    slot = sl % 2
                    cx.dma("sp", wst[slot][:], wv[:, :, sl * 512:(sl + 1) * 512], writes=[("wst", slot)])
                    cx.dma("sp", bst[slot][:], b_ada[l:l + 1, sl * 512:(sl + 1) * 512], writes=[("bst", slot)])
                    pr = prow[slot]
                    for k in range(NKC):
                        cx.op("pe", lambda e, k=k: e.matmul(pr[0:1, :], lhsT=cact[:, k:k + 1], rhs=wst[slot][:, k, :], start=(k == 0), stop=False),
                              reads=[("wst", slot), "cact"], writes=[("prow", slot)])
                    cx.op("pe", lambda e: e.matmul(pr[0:1, :], lhsT=one11, rhs=bst[slot][0:1, :], start=False, stop=True),
                          reads=[("bst", slot), "cf"], writes=[("prow", slot)])
                    cx.op("act", lambda e: e.activation(out=adarow[slot][:], in_=pr[0:1, :], func=AF.Copy), reads=[("prow", slot)], writes=[("adarow", slot)])
                    for cc in range(4):
                        col = sl * 4 + cc
                        cx.op("pe", lambda e, cc=cc, col=col: e.matmul(pa[:, col:col + 1], lhsT=adarow[slot][0:1, cc * 128:(cc + 1) * 128], rhs=one11, start=True, stop=True),
                              reads=[("adarow", slot), "cf"], writes=["pa"])
                cx.op("dve", lambda e, l=l: e.tensor_copy(out=mod[:, l, :], in_=pa[:, 0:144]), reads=["pa"], writes=["mod"])
                for j, (a, b) in {1: (1.0, 1.0), 4: (1.0, 1.0), 7: (1.0, 1.0), 2: (0.5, 0.5), 8: (0.5, 0.5), 5: (1.0, 1.0)}.items():
                    cx.op("dve", lambda e, l=l, j=j, a=a, b=b: e.tensor_scalar(out=mod[:, l, j * 16:(j + 1) * 16], in0=mod[:, l, j * 16:(j + 1) * 16], scalar1=a, scalar2=b, op0=ALU.mult, op1=ALU.add),
                          reads=["mod"], writes=["mod"])
            cx.barrier()

        def resid_ln_tail(l, lni, mo, psY, gcols, xt_slot, xt, yg, psT, lng, lnb, st, mv, rs, dst, tt):
            ygs = yg[mo % 2]
            cx.op("act", lambda e: e.activation(out=ygs[:], in_=psY[:], func=AF.Identity, scale=gcols[:, mo:mo + 1]), reads=[("psY", id(psY))], writes=[("yg", mo % 2)])
            pt = psT[mo % 2]
            for j in range(4):
                cx.op("pe", lambda e, j=j: e.transpose(out=pt[:, j * 128:(j + 1) * 128], in_=ygs[:, j * 128:(j + 1) * 128], identity=ident), reads=[("yg", mo % 2), "cf"], writes=[("psT", mo % 2)])
            xv = xt[xt_slot][:, :, mo * 128:(mo + 1) * 128]
            cx.op("dve", lambda e: e.scalar_tensor_tensor(out=xv, in0=xv, scalar=float(ALPHA), in1=pt[:].rearrange("p (j c) -> p j c", c=128), op0=ALU.mult, op1=ALU.add),
                  reads=[("psT", mo % 2), ("xt", xt_slot)], writes=[("xt", xt_slot)])
            if mo < NKC - 1:
                return None
            return ln_gen(xt_slot, xt, lng, lnb, st, mv, rs, dst, tt)

        def ln_gen(xt_slot, xt, lng, lnb, st, mv, rs, dst, tt):
            for j in range(4):
                xr = xt[xt_slot][:, j, :]
                for i in range(4):
                    cx.op("dve", lambda e, i=i: e.bn_stats(out=st[:, i * 6:(i + 1) * 6], in_=xr[:, i * 512:(i + 1) * 512]), reads=[("xt", xt_slot)], writes=["st"])
                    yield
                cx.op("dve", lambda e: e.bn_aggr(out=mv[:], in_=st[:]), reads=["st"], writes=["mv"])
                yield
                cx.op("act", lambda e: e.activation(out=rs[:], in_=mv[:, 1:2], func=AF.Sqrt, bias=lneps, scale=1.0), reads=["mv", "cf"], writes=["rs"])
                yield
                cx.op("dve", lambda e: e.reciprocal(out=rs[:], in_=rs[:]), reads=["rs"], writes=["rs"])
                yield
                cx.op("dve", lambda e: e.tensor_scalar(out=xr, in0=xr, scalar1=mv[:, 0:1], scalar2=rs[:, 0:1], op0=ALU.subtract, op1=ALU.mult), reads=["mv", "rs", ("xt", xt_slot)], writes=[("xt", xt_slot)])
                yield
                cx.op("dve", lambda e: e.tensor_tensor(out=xr, in0=xr, in1=lng[:], op=ALU.mult), reads=["lng", ("xt", xt_slot)], writes=[("xt", xt_slot)])
                yield
                cx.op("dve", lambda e: e.tensor_tensor(out=xr, in0=xr, in1=lnb[:], op=ALU.add), reads=["lnb", ("xt", xt_slot)], writes=[("xt", xt_slot)])
                yield
            cx.dma("pool", dst[tt * TS:(tt + 1) * TS, :].rearrange("(j p) d -> p j d", p=128), xt[xt_slot][:], reads=[("xt", xt_slot)], writes=[("xs", tt)])
            yield

        def load_x(src, tt, xt, slot):
            cx.dma("sp", xt[slot][:], src[tt * TS:(tt + 1) * TS, :].rearrange("(j p) d -> p j d", p=128), reads=[("xs", tt)], writes=[("xt", slot)])

        def make_hT(xt, slot, psT, hT, sccols, shcols):
            for kc in range(NKC):
                pt = psT[kc % 2]
                for j in range(4):
                    cx.op("pe", lambda e, j=j: e.transpose(out=pt[:, j * 128:(j + 1) * 128], in_=xt[slot][:, j, kc * 128:(kc + 1) * 128], identity=ident), reads=[("xt", slot), "cf"], writes=[("psT", kc % 2)])
                cx.op("act", lambda e: e.activation(out=hT[:, kc, :], in_=pt[:], func=AF.Identity, bias=shcols[:, kc:kc + 1], scale=sccols[:, kc:kc + 1]), reads=[("psT", kc % 2), "mod"], writes=[("hT", kc)])

        def ffn(l, w, src, dst):
            j0 = 0 if w == 1 else 6
            shc = mod[:, l, j0 * 16:(j0 + 1) * 16]
            scc = mod[:, l, (j0 + 1) * 16:(j0 + 2) * 16]
            gc = mod[:, l, (j0 + 2) * 16:(j0 + 3) * 16]
            lni = 0 if w == 1 else 2
            with ExitStack() as fs:
                xt = [sb("xt%d" % i, [128, 4, D], F32, fs) for i in range(2)]
                hT = sb("hT", [128, NKC, TS], BF16, fs)
                gT = sb("gT", [128, NM, TS], BF16, fs)
                sg = [sb("sg%d" % i, [128, TS], F32, fs) for i in range(2)]
                yg = [sb("yg%d" % i, [128, TS], F32, fs) for i in range(2)]
                wa = [sb("wa%d" % i, [128, NKC, 256], BF16, fs) for i in range(2)]
                wo = [sb("wo%d" % i, [128, NM, 128], BF16, fs) for i in range(2)]
                lng = sb("lng", [128, D], F32, fs)
                lnb = sb("lnb", [128, D], F32, fs)
                st = sb("st", [128, 24], F32, fs)
                mv = sb("mv", [128, 2], F32, fs)
                rs = sb("rs", [128, 1], F32, fs)
                psT = [fs.enter_context(psum_tensor("psT%d" % i, [128, 512], F32)) for i in range(2)]
                psA = [fs.enter_context(psum_tensor("psA%d" % i, [128, 512], F32)) for i in range(4)]
                psY = [fs.enter_context(psum_tensor("psY%d" % i, [128, 512], F32)) for i in range(2)]
                cx.dma("sp", lng[:], ln_g[lni][l:l + 1, :].partition_broadcast(128), writes=["lng"])
                cx.dma("sp", lnb[:], ln_b[lni][l:l + 1, :].partition_broadcast(128), writes=["lnb"])
                stream = []
                for tt in range(NT):
                    stream += [("a", tt, m) for m in range(NM)] + [("b", tt, mo) for mo in range(NKC)]
                cnt = {"a": 0, "b": 0}
                slots = []
                for kind, tt, i in stream:
                    slots.append(cnt[kind] % 2)
                    cnt[kind] += 1

                def issue(si):
                    if si >= len(stream):
                        return
                    kind, tt, i = stream[si]
                    s_ = slots[si]
                    if kind == "a":
                        cx.dma("sp", wa[s_][:], wi_s[(l, w)][i], reads=[("wi", l, w, i)], writes=[("wa", s_)])
                    else:
                        cx.dma("sp", wo[s_][:], wo_s[(l, w)][i], reads=[("wo", l, w, i)], writes=[("wo", s_)])

                load_x(src, 0, xt, 0)
                issue(0)
                si = 0
                pend = [None, None]

                def adv(n):
                    for _ in range(n):
                        if pend[0] is not None:
                            try:
                                next(pend[0])
                            except StopIteration:
                                pend[0] = None
                        if pend[0] is None and pend[1] is not None:
                            load_x(src, pend[1], xt, pend[1] % 2)
                            pend[1] = None
                for tt in range(NT):
                    xs_ = tt % 2
                    if tt + 1 < NT:
                        pend[1] = tt + 1
                        if pend[0] is None:
                            adv(1)
                    make_hT(xt, xs_, psT, hT, scc, shc)
                    for m in range(NM):
                        issue(si + 1)
                        s_ = slots[si]
                        pa_, pb_ = psA[(m % 2) * 2], psA[(m % 2) * 2 + 1]
                        for k in range(NKC):
                            cx.op("pe", lambda e, k=k: e.matmul(pa_[:], lhsT=wa[s_][:, k, 0:128], rhs=hT[:, k, :], start=(k == 0), stop=(k == NKC - 1)),
                                  reads=[("wa", s_), ("hT", k)], writes=[("psA", (m % 2) * 2)])
                        for k in range(NKC):
                            cx.op("pe", lambda e, k=k: e.matmul(pb_[:], lhsT=wa[s_][:, k, 128:256], rhs=hT[:, k, :], start=(k == 0), stop=(k == NKC - 1)),
                                  reads=[("wa", s_), ("hT", k)], writes=[("psA", (m % 2) * 2 + 1)])
                        cx.op("act", lambda e: e.activation(out=sg[m % 2][:], in_=pa_[:], func=AF.Silu), reads=[("psA", (m % 2) * 2)], writes=[("sg", m % 2)])
                        cx.op("dve", lambda e: e.tensor_tensor(out=gT[:, m, :], in0=sg[m % 2][:], in1=pb_[:], op=ALU.mult), reads=[("sg", m % 2), ("psA", (m % 2) * 2 + 1)], writes=[("gT", m)])
                        si += 1
                        adv(4)
                        if m % 4 == 0:
                            tick()
                    adv(1000)
                    for mo in range(NKC):
                        issue(si + 1)
                        s_ = slots[si]
                        py = psY[mo % 2]
                        for k in range(NM):
                            cx.op("pe", lambda e, k=k: e.matmul(py[:], lhsT=wo[s_][:, k, :], rhs=gT[:, k, :], start=(k == 0), stop=(k == NM - 1)),
                                  reads=[("wo", s_), ("gT", k)], writes=[("psY", id(py))])
                        g_ = resid_ln_tail(l, lni, mo, py, gc, xs_, xt, yg, psT, lng, lnb, st, mv, rs, dst, tt)
                        if g_ is not None:
                            pend[0] = g_
                        si += 1
                        tick()
                adv(1000)
                cx.barrier()

        def mixer(l, src, dst):
            shc = mod[:, l, 48:64]
            scc = mod[:, l, 64:80]
            gc = mod[:, l, 80:96]
            with ExitStack() as fs:
                xt = [sb("xt%d" % i, [128, 4, D], F32, fs) for i in range(2)]
                hTb = [sb("hTb%d" % i, [128, NKC, TS], BF16, fs) for i in range(2)]
                psT = [fs.enter_context(psum_tensor("psT%d" % i, [128, 512], F32)) for i in range(2)]
                load_x(src, 0, xt, 0)
                for tt in range(NT):
                    if tt + 1 < NT:
                        load_x(src, tt + 1, xt, (tt + 1) % 2)
                    hT = hTb[tt % 2]
                    for kc in range(NKC):
                        pt = psT[kc % 2]
                        for j in range(4):
                            cx.op("pe", lambda e, j=j: e.transpose(out=pt[:, j * 128:(j + 1) * 128], in_=xt[tt % 2][:, j, kc * 128:(kc + 1) * 128], identity=ident), reads=[("xt", tt % 2), "cf"], writes=[("psT", kc % 2)])
                        cx.op("act", lambda e: e.activation(out=hT[:, kc, :], in_=pt[:], func=AF.Identity, bias=shc[:, kc:kc + 1], scale=scc[:, kc:kc + 1]), reads=[("psT", kc % 2), "mod"], writes=[("hTb", tt % 2)])
                    cx.dma("pool", hsc[:, :, tt * TS:(tt + 1) * TS], hT[:], reads=[("hTb", tt % 2)], writes=[("hsc", tt)])
                    tick(2)
                cx.barrier()

            def load_h(hTt, slot, tt):
                cx.dma("sp", hTt[slot][:], hsc[:, :, tt * TS:(tt + 1) * TS], reads=[("hsc", tt)], writes=[("hTt", slot)])

            def finalize(totv, key_tot, gcolv, gate, psR, osq, rt, yTt, u, tt, tmpk, gkey="gate", rkey="psR", lnexp=False):
                cx.op("act", lambda e: e.activation(out=osq[:], in_=totv, func=AF.Square), reads=[key_tot], writes=["osq"])
                yield
                cx.op("pe", lambda e: e.matmul(psR[:], lhsT=onesm, rhs=osq[:], start=True, stop=True), reads=["osq", "cf"], writes=[rkey])
                yield
                if lnexp:
                    cx.op("act", lambda e: e.activation(out=rt[:], in_=psR[:], func=AF.Ln, bias=rmseps, scale=1.0), reads=[rkey, "cf"], writes=["rt"])
                    yield
                    cx.op("act", lambda e: e.activation(out=rt[:], in_=rt[:], func=AF.Exp, scale=-0.5), reads=["rt"], writes=["rt"])
                    yield
                else:
                    cx.op("act", lambda e: e.activation(out=rt[:], in_=psR[:], func=AF.Sqrt, bias=rmseps, scale=1.0), reads=[rkey, "cf"], writes=["rt"])
                    yield
                    cx.op("dve", lambda e: e.reciprocal(out=rt[:], in_=rt[:]), reads=["rt"], writes=["rt"])
                    yield
                cx.op("dve", lambda e: e.tensor_tensor(out=rt[:], in0=rt[:], in1=totv, op=ALU.mult), reads=["rt", key_tot], writes=["rt"])
                yield
                ys = yTt[tmpk % 2]
                if gate is None:
                    cx.op("dve", lambda e: e.tensor_scalar(out=ys[:], in0=rt[:], scalar1=gcolv, scalar2=None, op0=ALU.mult), reads=["rt", "gcol"], writes=[("yTt", tmpk % 2)])
                    yield
                else:
                    cx.op("dve", lambda e: e.scalar_tensor_tensor(out=ys[:], in0=rt[:], scalar=gcolv, in1=gate, op0=ALU.mult, op1=ALU.mult), reads=["rt", "gcol", gkey], writes=[("yTt", tmpk % 2)])
                    yield
                cx.dma("pool", ysc[:, u, tt * TS:(tt + 1) * TS], ys[:], reads=[("yTt", tmpk % 2)], writes=[("ysc", u, tt)])
                yield

            with ExitStack() as fs:
                hTt = [sb("hTt%d" % i, [128, NKC, TS], BF16, fs) for i in range(2)]
                wu = sb("wuf", [128, NKC, 4, 128], BF16, fs)
                Wg = sb("Wg", [128, 4, S // 128, 256], BF16, fs)
                UTb = [sb("UTb%d" % i, [128, TS], BF16, fs) for i in range(2)]
                HS = S // 2
                csls = [sb("csl%d" % i, [128, S // 128, 256], BF16, fs) for i in range(2)]
                ssls = [sb("ssl%d" % i, [128, S // 128, 256], BF16, fs) for i in range(2)]
                osq = sb("osq", [128, TS], F32, fs)
                rt = sb("rt", [128, TS], F32, fs)
                ysb = sb("ysb", [128, TS], F32, fs)
                yTt = [sb("yTt%d" % i, [128, TS], BF16, fs) for i in range(2)]
                psP = [fs.enter_context(psum_tensor("psP%d" % i, [128, 512], F32)) for i in range(2)]
                psW = [fs.enter_context(psum_tensor("psW%d" % i, [128, 1024], F32)) for i in range(1)]
                psYf = [fs.enter_context(psum_tensor("psYf%d" % i, [128, 512], F32)) for i in range(2)]
                psR = fs.enter_context(psum_tensor("psR", [128, 512], F32))
                for g in range(4):
                    cx.dma("sp", wu[:, :, g, :], win_s[l][g], reads=[("win", l, g)], writes=["wu"])
                scl = 1.0 / float(np.sqrt(S * 128.0))
                load_h(hTt, 0, 0)
                for tt in range(NT):
                    if tt + 1 < NT:
                        load_h(hTt, (tt + 1) % 2, tt + 1)
                    for g in range(4):
                        ki = (tt * 4 + g)
                        pp_ = psP[ki % 2]
                        for k in range(NKC):
                            cx.op("pe", lambda e, k=k: e.matmul(pp_[:], lhsT=wu[:, k, g, :], rhs=hTt[tt % 2][:, k, :], start=(k == 0), stop=(k == NKC - 1)), reads=["wu", ("hTt", tt % 2)], writes=[("psP", ki % 2)])
                        ub = UTb[ki % 2]
                        cx.op("act", lambda e: e.activation(out=ub[:], in_=pp_[:], func=AF.Copy, scale=scl), reads=[("psP", ki % 2)], writes=[("UTb", ki % 2)])
                        pw = psW[0]
                        for j in range(4):
                            cx.op("pe", lambda e, j=j: e.matmul(pw[:, j * 256:(j + 1) * 256], lhsT=ub[:, j * 128:(j + 1) * 128], rhs=csg, start=True, stop=True), reads=[("UTb", ki % 2), "cb"], writes=["psW"])
                        cx.op("dve", lambda e: e.tensor_copy(out=Wg[:, g, tt * 4:(tt + 1) * 4, :], in_=pw[:].rearrange("p (j c) -> p j c", c=256)), reads=["psW"], writes=["Wg"])
                    tick(2)
                NB = S // 128
                PSL = 256
                fk = 0
                def load_dft(psl):
                    cx.dma("sp", csls[psl % 2][:], dftc.rearrange("(b p) n -> p b n", p=128)[:, :, psl * PSL:(psl + 1) * PSL], writes=[("csl", psl % 2)])
                    cx.dma("sp", ssls[psl % 2][:], dfts.rearrange("(b p) n -> p b n", p=128)[:, :, psl * PSL:(psl + 1) * PSL], writes=[("ssl", psl % 2)])
                load_dft(0)
                for psl in range(S // PSL):
                    if psl + 1 < S // PSL:
                        load_dft(psl + 1)
                    csl = csls[psl % 2]
                    ssl = ssls[psl % 2]
                    for g in range(4):
                        py = psYf[fk % 2]
                        for b_ in range(NB):
                            cx.op("pe", lambda e, b_=b_: e.matmul(py[:, 0:PSL], lhsT=Wg[:, g, b_, 0:128], rhs=csl[:, b_, :], start=(b_ == 0), stop=False), reads=["Wg", ("csl", psl % 2)], writes=[("psYf", fk % 2)])
                            cx.op("pe", lambda e, b_=b_: e.matmul(py[:, 0:PSL], lhsT=Wg[:, g, b_, 128:256], rhs=ssl[:, b_, :], start=False, stop=(b_ == NB - 1)), reads=["Wg", ("ssl", psl % 2)], writes=[("psYf", fk % 2)])
                        cx.op("act", lambda e: e.activation(out=ysb[:, 0:PSL], in_=py[:, 0:PSL], func=AF.Copy), reads=[("psYf", fk % 2)], writes=["ysb"])
                        cx.op("act", lambda e: e.activation(out=osq[:, 0:PSL], in_=ysb[:, 0:PSL], func=AF.Square), reads=["ysb"], writes=["osq"])
                        cx.op("pe", lambda e: e.matmul(psR[:, 0:PSL], lhsT=onesm, rhs=osq[:, 0:PSL], start=True, stop=True), reads=["osq", "cf"], writes=["psR"])
                        cx.op("act", lambda e: e.activation(out=rt[:, 0:PSL], in_=psR[:, 0:PSL], func=AF.Sqrt, bias=rmseps, scale=1.0), reads=["psR", "cf"], writes=["rt"])
                        cx.op("dve", lambda e: e.reciprocal(out=rt[:, 0:PSL], in_=rt[:, 0:PSL]), reads=["rt"], writes=["rt"])
                        cx.op("dve", lambda e: e.tensor_tensor(out=rt[:, 0:PSL], in0=rt[:, 0:PSL], in1=ysb[:, 0:PSL], op=ALU.mult), reads=["rt", "ysb"], writes=["rt"])
                        ys = yTt[fk % 2]
                        cx.op("dve", lambda e: e.tensor_scalar(out=ys[:, 0:PSL], in0=rt[:, 0:PSL], scalar1=gcol[:, l * 4 + g:l * 4 + g + 1], scalar2=None, op0=ALU.mult), reads=["rt", "gcol"], writes=[("yTt", fk % 2)])
                        cx.dma("pool", ysc[:, g, psl * PSL:(psl + 1) * PSL], ys[:, 0:PSL], reads=[("yTt", fk % 2)], writes=[("ysc", g, psl)])
                        fk += 1
                    tick(2)
                cx.barrier()

            with ExitStack() as fs:
                hTt = [sb("hTt%d" % i, [128, NKC, TS], BF16, fs) for i in range(2)]
                wus = [sb("wuh%d" % i, [128, NKC, 5, 128], BF16, fs) for i in range(2)]
                qs = sb("qs", [128, S], BF16, fs)
                vtok = sb("vtok", [128, S // 128, 128], BF16, fs)
                gate = sb("gate", [128, S], BF16, fs)
                zb = sb("zb", [128, S], F32, fs)
                oacc = sb("oacc", [128, S], F32, fs)
                tmps = [[sb("tmp%d_%d" % (b_, i), [128, TS], F32, fs) for i in range(6)] for b_ in range(2)]
                vTt = sb("vTt", [128, TS], BF16, fs)
                qins = [sb("qin%d" % i, [128, TS], BF16, fs) for i in range(2)]
                kins = [sb("kin%d" % i, [128, TS], BF16, fs) for i in range(2)]
                qsts = [sb("qst%d" % i, [128, TS], BF16, fs) for i in range(2)]
                ksts = [sb("kst%d" % i, [128, TS], BF16, fs) for i in range(2)]
                Am = sb("Am", [128, TS], BF16, fs)
                ktok = sb("ktok", [128, TS], BF16, fs)
                Sbfs = [sb("Sbf%d" % i, [128, 8, 128], BF16, fs) for i in range(2)]
                Sr = sb("Sr", [128, 9, 128], F32, fs)
                osq = sb("osq", [128, TS], F32, fs)
                rt = sb("rt", [128, TS], F32, fs)
                totb = sb("totb", [128, TS], F32, fs)
                yTt = [sb("yTt%d" % i, [128, TS], BF16, fs) for i in range(2)]
                psP = [fs.enter_context(psum_tensor("psP%d" % i, [128, 512], F32)) for i in range(3)]
                psAA = fs.enter_context(psum_tensor("psAA", [128, 512], F32))
                psK = fs.enter_context(psum_tensor("psK", [128, 1024], BF16))
                psO = fs.enter_context(psum_tensor("psO", [128, 512], F32))
                psU = [fs.enter_context(psum_tensor("psU%d" % i, [128, 512], F32)) for i in range(2)]
                psR = psP[2]
                fcount = [0]

                def load_wu(h):
                    for ci in range(5):
                        cc = 4 + ci * NH + h
                        cx.dma("sp", wus[h % 2][:, :, ci, :], win_s[l][cc], reads=[("win", l, cc)], writes=[("wu", h % 2)])

                def P(h, tt):
                    wu = wus[h % 2]
                    bs = tt % 2
                    if tt + 1 < NT:
                        load_h(hTt, (tt + 1) % 2, tt + 1)
                    tsl = slice(tt * TS, (tt + 1) * TS)

                    def proj(ci, bank):
                        for k in range(NKC):
                            cx.op("pe", lambda e, k=k: e.matmul(psP[bank][:], lhsT=wu[:, k, ci, :], rhs=hTt[tt % 2][:, k, :], start=(k == 0), stop=(k == NKC - 1)),
                                  reads=[("wu", h % 2), ("hTt", tt % 2)], writes=[("psP", bank)])
                            yield
                    yield from proj(0, 0)
                    yield from proj(4, 1)
                    yield from proj(2, 2)
                    cx.op("act", lambda e: e.activation(out=qs[:, tsl], in_=psP[0][:], func=AF.Silu), reads=[("psP", 0)], writes=[("qs", tt)])
                    yield
                    cx.op("act", lambda e: e.activation(out=gate[:, tsl], in_=psP[1][:], func=AF.Silu), reads=[("psP", 1)], writes=[("gate", tt)])
                    yield
                    cx.op("act", lambda e: e.activation(out=tmps[bs][0][:], in_=psP[2][:], func=AF.Exp, scale=-1.0), reads=[("psP", 2)], writes=[("t1", bs)])
                    yield
                    yield from proj(1, 0)
                    yield from proj(3, 1)
                    cx.op("act", lambda e: e.activation(out=vTt[:], in_=psP[0][:], func=AF.Copy), reads=[("psP", 0)], writes=["vTt"])
                    yield
                    cx.op("dve", lambda e: e.tensor_copy(out=zb[:, tsl], in_=psP[1][:]), reads=[("psP", 1)], writes=[("zb", tt)])
                    yield
                    for j in range(4):
                        cx.op("pe", lambda e, j=j: e.transpose(out=psK[:, 512 + j * 128:512 + (j + 1) * 128], in_=vTt[:, j * 128:(j + 1) * 128], identity=identb), reads=["vTt", "cb"], writes=["psK"])
                        yield
                    cx.op("dve", lambda e: e.tensor_copy(out=vtok[:, tt * 4:(tt + 1) * 4, :], in_=psK[:, 512:1024].rearrange("p (j c) -> p j c", c=128)), reads=["psK"], writes=[("vtok", tt)])
                    yield

                def Fr(h, tt, dr, bs):
                    lcol = lbc[:, l, dr * 12 + h:dr * 12 + h + 1]
                    ocol = omlc[:, l, dr * 12 + h:dr * 12 + h + 1]
                    t1, t2, t3, t4, t5, t6 = tmps[bs]
                    K1, K2, K3, K4, K5, K6 = [("t%d" % (i + 1), bs) for i in range(6)]
                    qin, kin, qst, kst = qins[bs], kins[bs], qsts[bs], ksts[bs]
                    tsl = slice(tt * TS, (tt + 1) * TS)
                    qv = qs[:, tsl]
                    QK = ("qs", tt)
                    if dr == 1:
                        cx.op("act", lambda e: e.activation(out=t1[:], in_=zb[:, tsl], func=AF.Exp, scale=-1.0), reads=[("zb", tt)], writes=[K1])
                        yield
                    cx.op("act", lambda e: e.activation(out=t1[:], in_=t1[:], func=AF.Ln, bias=cf[:, 768:769], scale=1.0), reads=[K1, "cf"], writes=[K1])
                    yield
                    cx.op("act", lambda e: e.activation(out=t1[:], in_=t1[:], func=AF.Exp, scale=-1.0), reads=[K1], writes=[K1])
                    yield
                    cx.op("dve", lambda e: e.tensor_scalar(out=t1[:], in0=t1[:], scalar1=ocol, scalar2=lcol, op0=ALU.mult, op1=ALU.add), reads=[K1, "lbc"], writes=[K1])
                    yield
                    cx.op("act", lambda e: e.activation(out=t2[:], in_=t1[:], func=AF.Ln), reads=[K1], writes=[K2])
                    yield
                    cx.op("dve", lambda e: e.tensor_scalar(out=t1[:], in0=t1[:], scalar1=-1.0, scalar2=1.0, op0=ALU.mult, op1=ALU.add), reads=[K1], writes=[K1])
                    yield
                    cx.op("dve", lambda e: e.tensor_tensor_scan(out=t3[:], data0=rmask, data1=t2[:], initial=0.0, op0=ALU.mult, op1=ALU.add), reads=[K2, "cf"], writes=[K3])
                    yield
                    if dr == 0:
                        cbuf, ckey, r = t3, K3, 32
                    else:
                        cx.op("dve", lambda e: e.tensor_tensor(out=t2[:], in0=t3[:], in1=t2[:], op=ALU.subtract), reads=[K2, K3], writes=[K2])
                        yield
                        cbuf, ckey, r = t2, K2, 31
                    c3 = cbuf[:].rearrange("p (n c) -> p n c", c=CH)
                    cum3 = t3[:].rearrange("p (n c) -> p n c", c=CH)

                    def v3(tb):
                        return tb[:].rearrange("p (n c) -> p n c", c=CH)
                    cx.op("dve", lambda e: e.tensor_tensor(out=v3(t4), in0=c3, in1=c3[:, :, r:r + 1].to_broadcast([128, 8, CH]), op=ALU.subtract), reads=[ckey], writes=[K4])
                    yield
                    sg_ = 1.0 if dr == 0 else -1.0
                    cx.op("act", lambda e: e.activation(out=t5[:], in_=t4[:], func=AF.Exp, scale=sg_), reads=[K4], writes=[K5])
                    yield
                    cx.op("act", lambda e: e.activation(out=t4[:], in_=t4[:], func=AF.Exp, scale=-sg_), reads=[K4], writes=[K4])
                    yield
                    cx.op("dve", lambda e: e.tensor_tensor(out=qin[:], in0=qv, in1=t5[:], op=ALU.mult), reads=[QK, K5], writes=[("qin", bs)])
                    yield
                    cx.op("dve", lambda e: e.tensor_tensor(out=kin[:], in0=t1[:], in1=t4[:], op=ALU.mult), reads=[K1, K4], writes=[("kin", bs)])
                    yield
                    if dr == 0:
                        cx.op("dve", lambda e: e.tensor_tensor(out=v3(t6), in0=c3, in1=c3[:, :, CH - 1:CH].to_broadcast([128, 8, CH]), op=ALU.subtract), reads=[K3], writes=[K6])
                        yield
                        cx.op("act", lambda e: e.activation(out=t3[:], in_=t3[:], func=AF.Exp), reads=[K3], writes=[K3])
                        yield
                        cx.op("act", lambda e: e.activation(out=t6[:], in_=t6[:], func=AF.Exp, scale=-1.0), reads=[K6], writes=[K6])
                        yield
                        cx.op("dve", lambda e: e.tensor_tensor(out=qst[:], in0=qv, in1=t3[:], op=ALU.mult), reads=[QK, K3], writes=[("qst", bs)])
                        yield
                        cx.op("dve", lambda e: e.tensor_tensor(out=kst[:], in0=t1[:], in1=t6[:], op=ALU.mult), reads=[K1, K6], writes=[("kst", bs)])
                        yield
                    else:
                        cx.op("dve", lambda e: e.tensor_tensor(out=v3(t6), in0=c3, in1=cum3[:, :, CH - 1:CH].to_broadcast([128, 8, CH]), op=ALU.subtract), reads=[K2, K3], writes=[K6])
                        yield
                        cx.op("act", lambda e: e.activation(out=t6[:], in_=t6[:], func=AF.Exp, scale=-1.0), reads=[K6], writes=[K6])
                        yield
                        cx.op("act", lambda e: e.activation(out=t2[:], in_=t2[:], func=AF.Exp), reads=[K2], writes=[K2])
                        yield
                        cx.op("act", lambda e: e.activation(out=t3[:], in_=t3[:], func=AF.Exp), reads=[K3], writes=[K3])
                        yield
                        cx.op("dve", lambda e: e.tensor_tensor(out=qst[:], in0=qv, in1=t6[:], op=ALU.mult), reads=[QK, K6], writes=[("qst", bs)])
                        yield
                        cx.op("dve", lambda e: e.tensor_tensor(out=kst[:], in0=t1[:], in1=t2[:], op=ALU.mult), reads=[K1, K2], writes=[("kst", bs)])
                        yield

                def B1(h, tt, dr, bs):
                    t3 = tmps[bs][2]
                    K3 = ("t3", bs)
                    qin, kin, qst, kst = qins[bs], kins[bs], qsts[bs], ksts[bs]
                    Sbf = Sbfs[bs]
                    d3 = t3[:].rearrange("p (n c) -> p n c", c=CH)
                    VK = ("vtok", tt)
                    for j in range(4):
                        cx.op("pe", lambda e, j=j: e.matmul(psAA[:, j * 128:(j + 1) * 128], lhsT=kin[:, j * 128:(j + 1) * 128], rhs=qin[:, j * 128:(j + 1) * 128], start=True, stop=True), reads=[("kin", bs), ("qin", bs)], writes=["psAA"])
                        yield
                    mk = maskF if dr == 0 else maskB
                    cx.op("dve", lambda e: e.tensor_tensor(out=Am[:], in0=psAA[:], in1=mk, op=ALU.mult), reads=["psAA", "cb"], writes=["Am"])
                    yield
                    for j in range(4):
                        cx.op("pe", lambda e, j=j: e.transpose(out=psK[:, j * 128:(j + 1) * 128], in_=kst[:, j * 128:(j + 1) * 128], identity=identb), reads=[("kst", bs), "cb"], writes=["psK"])
                        yield
                    cx.op("act", lambda e: e.activation(out=ktok[:], in_=psK[:, 0:512], func=AF.Copy), reads=["psK"], writes=["ktok"])
                    yield
                    for hf in range(2):
                        for j in range(4):
                            pu = psU[hf]
                            cx.op("pe", lambda e, j=j, hf=hf, pu=pu: e.matmul(pu[:, j * 128:(j + 1) * 128], lhsT=ktok[hf * 64:(hf + 1) * 64, j * 128:(j + 1) * 128], rhs=vtok[hf * 64:(hf + 1) * 64, tt * 4 + j, :], start=True, stop=True),
                                  reads=["ktok", VK], writes=[("psU", hf)])
                            yield

                def CHN(h, tt, dr, bs):
                    t3 = tmps[bs][2]
                    K3 = ("t3", bs)
                    Sbf = Sbfs[bs]
                    d3 = t3[:].rearrange("p (n c) -> p n c", c=CH)
                    order = list(range(8)) if dr == 0 else list(range(7, -1, -1))
                    for i, n in enumerate(order):
                        pu = psU[n % 2]
                        cx.op("act", lambda e, n=n, i=i: e.activation(out=Sbf[:, n, :], in_=Sr[:, i, :], func=AF.Copy), reads=[("Sr", i)], writes=[("Sbf", bs, n)])
                        yield
                        cx.op("dve", lambda e, n=n, i=i, pu=pu: e.scalar_tensor_tensor(out=Sr[:, i + 1, :], in0=Sr[:, i, :], scalar=d3[:, n, CH - 1:CH], in1=pu[:, (n // 2) * 128:(n // 2 + 1) * 128], op0=ALU.mult, op1=ALU.add),
                              reads=[("Sr", i), K3, ("psU", n % 2)], writes=[("Sr", i + 1)])
                        yield
                    cx.op("dve", lambda e: e.tensor_copy(out=Sr[:, 0, :], in_=Sr[:, 8, :]), reads=[("Sr", 8)], writes=[("Sr", 0)])
                    yield

                def B2(h, tt, dr, bs):
                    qst = qsts[bs]
                    Sbf = Sbfs[bs]
                    VK = ("vtok", tt)
                    for j in range(4):
                        cx.op("pe", lambda e, j=j: e.matmul(psO[:, j * 128:(j + 1) * 128], lhsT=vtok[:, tt * 4 + j, :], rhs=Am[:, j * 128:(j + 1) * 128], start=True, stop=False), reads=[VK, "Am"], writes=["psO"])
                        yield
                        for hf in range(2):
                            n = 2 * j + hf
                            cx.op("pe", lambda e, n=n, hf=hf: e.matmul(psO[:, n * 64:(n + 1) * 64], lhsT=Sbf[:, n, :], rhs=qst[:, n * 64:(n + 1) * 64], start=False, stop=(hf == 1)), reads=[("Sbf", bs, n), ("qst", bs)], writes=["psO"])
                            yield

                def run(*gens):
                    gens = [g for g in gens if g is not None]
                    while gens:
                        for g in list(gens):
                            try:
                                next(g)
                            except StopIteration:
                                gens.remove(g)

                def seq(*gens):
                    for g in gens:
                        yield from g

                def evac_f(tt):
                    tsl = slice(tt * TS, (tt + 1) * TS)
                    cx.op("act", lambda e: e.activation(out=oacc[:, tsl], in_=psO[:], func=AF.Copy), reads=["psO"], writes=[("oacc", tt)])
                    yield

                def evac_b(h, tt):
                    tsl = slice(tt * TS, (tt + 1) * TS)
                    cx.op("dve", lambda e: e.tensor_tensor(out=totb[:], in0=oacc[:, tsl], in1=psO[:], op=ALU.add), reads=[("oacc", tt), "psO"], writes=["totb"])
                    yield
                    yield from finalize(totb[:], "totb", gcol[:, 16 + l * NH + h:16 + l * NH + h + 1], gate[:, tsl], psR, osq, rt, yTt, 4 + h, tt, fcount[0], gkey=("gate", tt), rkey=("psP", 2), lnexp=True)
                    fcount[0] += 1

                load_wu(0)
                for h in range(NH):
                    if h + 1 < NH:
                        load_wu(h + 1)
                    cx.op("dve", lambda e: e.memset(Sr[:, 0, :], 0.0), writes=[("Sr", 0)])
                    load_h(hTt, 0, 0)
                    run(P(h, 0))
                    run(Fr(h, 0, 0, 0), P(h, 1) if NT > 1 else None)
                    for tt in range(NT):
                        bs = tt % 2
                        run(seq(B1(h, tt, 0, bs), CHN(h, tt, 0, bs), B2(h, tt, 0, bs), evac_f(tt)),
                            Fr(h, tt + 1, 0, (tt + 1) % 2) if tt + 1 < NT else None,
                            P(h, tt + 2) if tt + 2 < NT else None)
                        tick(1)
                    cx.op("dve", lambda e: e.memset(Sr[:, 0, :], 0.0), writes=[("Sr", 0)])
                    order_t = list(range(NT - 1, -1, -1))
                    run(Fr(h, order_t[0], 1, 0))
                    for idx, tt in enumerate(order_t):
                        bs = idx % 2
                        run(seq(B1(h, tt, 1, bs), CHN(h, tt, 1, bs), B2(h, tt, 1, bs), evac_b(h, tt)),
                            Fr(h, order_t[idx + 1], 1, (idx + 1) % 2) if idx + 1 < NT else None)
                        tick(1)
                cx.barrier()

            with ExitStack() as fs:
                xt = [sb("xt%d" % i, [128, 4, D], F32, fs) for i in range(2)]
                yTl = [sb("yTl%d" % i, [128, NU, TS], BF16, fs) for i in range(2)]
                wob = sb("wob", [128, NKC, D], BF16, fs)
                yg = [sb("yg%d" % i, [128, TS], F32, fs) for i in range(2)]
                lng = sb("lng", [128, D], F32, fs)
                lnb = sb("lnb", [128, D], F32, fs)
                st = sb("st", [128, 24], F32, fs)
                mv = sb("mv", [128, 2], F32, fs)
                rs = sb("rs", [128, 1], F32, fs)
                psT = [fs.enter_context(psum_tensor("psT%d" % i, [128, 512], F32)) for i in range(2)]
                psY = [fs.enter_context(psum_tensor("psY%d" % i, [128, 512], F32)) for i in range(2)]
                cx.dma("sp", lng[:], ln_g[1][l:l + 1, :].partition_broadcast(128), writes=["lng"])
                cx.dma("sp", lnb[:], ln_b[1][l:l + 1, :].partition_broadcast(128), writes=["lnb"])
                for k4 in range(4):
                    cx.dma("sp", wob[:, k4 * 4:(k4 + 1) * 4, :], wout_s[l][:, k4 * 4:(k4 + 1) * 4, :], reads=[("wout", l, k4)], writes=["wob"])

                def load_y(tt):
                    cx.dma("sp", yTl[tt % 2][:], ysc[:, :, tt * TS:(tt + 1) * TS], reads=[("ysc", u_, p_) for u_ in range(NU) for p_ in range(max(NT, S // 256))], writes=[("yTl", tt % 2)])
                load_x(src, 0, xt, 0)
                load_y(0)
                pend = [None, None]

                def adv(n):
                    for _ in range(n):
                        if pend[0] is not None:
                            try:
                                next(pend[0])
                            except StopIteration:
                                pend[0] = None
                        if pend[0] is None and pend[1] is not None:
                            load_x(src, pend[1], xt, pend[1] % 2)
                            pend[1] = None
                for tt in range(NT):
                    if tt + 1 < NT:
                        pend[1] = tt + 1
                        if pend[0] is None:
                            adv(1)
                        load_y(tt + 1)
                    for mo in range(NKC):
                        if mo == NKC - 1:
                            adv(1000)
                        py = psY[mo % 2]
                        for k in range(NU):
                            cx.op("pe", lambda e, k=k: e.matmul(py[:], lhsT=wob[:, k, mo * 128:(mo + 1) * 128], rhs=yTl[tt % 2][:, k, :], start=(k == 0), stop=(k == NU - 1)), reads=["wob", ("yTl", tt % 2)], writes=[("psY", id(py))])
                        g_ = resid_ln_tail(l, 1, mo, py, gc, tt % 2, xt, yg, psT, lng, lnb, st, mv, rs, dst, tt)
                        if g_ is not None:
                            pend[0] = g_
                        else:
                            adv(6)
                    tick(4)
                adv(1000)
                cx.barrier()

        nplan = len(plan)
        for pi, p in enumerate(plan):
            for f in cast_list(p[0], p[1], p[2] if p[0] == "ffn" else None):
                pending_casts.append((pi, f))
        for pi, p in enumerate(plan):
            src = x_in if pi == 0 else xs
            dst = y_out if pi == nplan - 1 else xs
            while pending_casts and pending_casts[0][0] <= pi:
                pending_casts.pop(0)[1]()
            if p[0] == "ffn":
                ffn(p[1], p[2], src, dst)
            else:
                mixer(p[1], src, dst)
        while pending_casts:
            tick()
        cx.finish()
    return nc


_CACHE = {}


def kernel(**inputs):
    S = inputs["x"].shape[1]
    B = inputs["x"].shape[0]
    plan = []
    for l in range(DEPTH):
        plan += [("ffn", l, 1), ("mix", l), ("ffn", l, 3)]
    key = (S, tuple(plan))
    if key not in _CACHE:
        _CACHE[key] = build(S, plan, list(range(DEPTH)))
    nc = _CACHE[key]
    cf, cb, dc, ds = _consts(S)
    in_maps = []
    for b in range(B):
        m = {k: np.ascontiguousarray(v) for k, v in inputs.items() if k not in ("x", "c")}
        m["x"] = np.ascontiguousarray(inputs["x"][b])
        m["c"] = np.ascontiguousarray(inputs["c"][b:b + 1])
        m["cst_f"] = cf
        m["cst_b"] = cb
        m["dft_c"] = dc
        m["dft_s"] = ds
        in_maps.append(m)
    res = run_bass_kernel_spmd(nc, in_maps, core_ids=list(range(B)))
    return np.stack([np.asarray(r["y"]) for r in res.results], axis=0).astype(np.float32)
```
